# Optimizing a Trainium2 kernel written in Bass

```python
import jax, jax.numpy as jnp
from jax import lax
import numpy as np

D_MODEL = 1024
BATCH = 2
SEQ = 8192
DEPTH = 2

D_MIX = D_MODEL
CONV_WIDTH = D_MIX // 4
CONV_GROUPS = 4
CONV_K = 31
MLA_HEADS = 8
MLA_NOPE = 64
MLA_ROPE = 32
MLA_QK = MLA_NOPE + MLA_ROPE
MLA_V = 64
MLA_WIDTH = MLA_HEADS * MLA_V
Q_LORA = 768
KV_LORA = 256
ROPE_THETA = 10000.0
Q_BLOCK = 128
SG_WIDTH = D_MIX - CONV_WIDTH - MLA_WIDTH
SG_HEADS = 4
SG_HEAD_DIM = SG_WIDTH // SG_HEADS
SG_CHUNK = 128
IN_SIZES = (CONV_WIDTH, CONV_WIDTH, CONV_WIDTH,
            Q_LORA, KV_LORA, MLA_ROPE, MLA_WIDTH,
            SG_WIDTH, SG_WIDTH, SG_WIDTH)
IN_COLS = 3 * CONV_WIDTH + Q_LORA + KV_LORA + MLA_ROPE + MLA_WIDTH + 3 * SG_WIDTH
EPS = 1e-6

kernel_name = 'hybrid_conv_mla_sgu_parallel_heads'


def _rms_norm(x, g):
    xf = x.astype(jnp.float32)
    y = xf * lax.rsqrt(jnp.mean(xf * xf, axis=-1, keepdims=True) + EPS)
    return (y * g.astype(jnp.float32)).astype(x.dtype)


def _layer_norm(x, g, b):
    xf = x.astype(jnp.float32)
    mu = jnp.mean(xf, axis=-1, keepdims=True)
    var = jnp.mean(jnp.square(xf - mu), axis=-1, keepdims=True)
    y = (xf - mu) * lax.rsqrt(var + EPS) * g.astype(jnp.float32) + b.astype(jnp.float32)
    return y.astype(x.dtype)


def _rope_tables(seq):
    half = MLA_ROPE // 2
    inv_freq = ROPE_THETA ** (-jnp.arange(half, dtype=jnp.float32) / half)
    ang = jnp.arange(seq, dtype=jnp.float32)[:, None] * inv_freq[None, :]
    return jnp.cos(ang), jnp.sin(ang)


def _apply_rope(x, cos, sin):
    half = MLA_ROPE // 2
    c = cos[None, :, None, :].astype(x.dtype)
    s = sin[None, :, None, :].astype(x.dtype)
    x1, x2 = x[..., :half], x[..., half:]
    return jnp.concatenate([x1 * c - x2 * s, x1 * s + x2 * c], axis=-1)


def _conv_branch(a, a_glu, conv_w, conv_b, ln_g, ln_b, pw_w, pw_b):
    y = a * jax.nn.sigmoid(a_glu)
    y = lax.conv_general_dilated(
        y, conv_w[:, None, :], window_strides=(1,),
        padding=[(CONV_K - 1, 0)],
        dimension_numbers=('NWC', 'WIO', 'NWC'),
        feature_group_count=CONV_WIDTH) + conv_b
    y = jax.nn.silu(_layer_norm(y, ln_g, ln_b))
    return y @ pw_w + pw_b


def _causal_attention(q, k, v):
    B, S, H, Dq = q.shape
    nb = S // Q_BLOCK
    scale = Dq ** -0.5
    q_blocks = jnp.moveaxis(q.reshape(B, nb, Q_BLOCK, H, Dq), 1, 0)
    k_pos = jnp.arange(S)

    def block(args):
        qb, i = args
        q_pos = i * Q_BLOCK + jnp.arange(Q_BLOCK)
        s = jnp.einsum('bqhd,bkhd->bhqk', qb, k).astype(jnp.float32) * scale
        s = jnp.where(k_pos[None, :] <= q_pos[:, None], s, -jnp.inf)
        p = jax.nn.softmax(s, axis=-1).astype(v.dtype)
        return jnp.einsum('bhqk,bkhd->bqhd', p, v)

    out = lax.map(block, (q_blocks, jnp.arange(nb)))
    return jnp.moveaxis(out, 0, 1).reshape(B, S, H * v.shape[-1])


def _mla_branch(c_q, c_kv, k_rope, q_norm_g, w_uq, kv_norm_g, w_ukv,
                qk_q_g, qk_k_g, cos, sin):
    B, S, _ = c_q.shape
    q = (_rms_norm(c_q, q_norm_g) @ w_uq).reshape(B, S, MLA_HEADS, MLA_QK)
    kv = (_rms_norm(c_kv, kv_norm_g) @ w_ukv).reshape(B, S, MLA_HEADS, MLA_NOPE + MLA_V)
    k_nope, v = kv[..., :MLA_NOPE], kv[..., MLA_NOPE:]
    k_r = jnp.broadcast_to(k_rope[:, :, None, :], (B, S, MLA_HEADS, MLA_ROPE))
    k = jnp.concatenate([k_nope, k_r], axis=-1)
    q = _rms_norm(q, qk_q_g)
    k = _rms_norm(k, qk_k_g)
    q = jnp.concatenate([q[..., :MLA_NOPE], _apply_rope(q[..., MLA_NOPE:], cos, sin)], axis=-1)
    k = jnp.concatenate([k[..., :MLA_NOPE], _apply_rope(k[..., MLA_NOPE:], cos, sin)], axis=-1)
    return _causal_attention(q, k, v)


def _sgu_branch(u, v, ln_g, ln_b, sg_w, sg_b):
    B, S, _ = u.shape
    nc = S // SG_CHUNK
    u = jax.nn.gelu(u)
    v = _layer_norm(jax.nn.gelu(v), ln_g, ln_b)
    v = v.reshape(B, nc, SG_CHUNK, SG_HEADS, SG_HEAD_DIM)
    mask = jnp.tril(jnp.ones((SG_CHUNK, SG_CHUNK), dtype=bool))
    w = jnp.where(mask[None], sg_w, jnp.zeros_like(sg_w))
    mixed = jnp.einsum('gts,bcsgd->bctgd', w, v) + sg_b.T[None, None, :, :, None]
    return u * mixed.reshape(B, S, SG_WIDTH)


def _layer(x, cos, sin, norm_g, w_in, conv_w, conv_b, conv_ln_g, conv_ln_b,
           conv_pw_w, conv_pw_b, q_norm_g, w_uq, kv_norm_g, w_ukv, qk_q_g, qk_k_g,
           sg_ln_g, sg_ln_b, sg_w, sg_b, branch_norm_g, w_out):
    h = _rms_norm(x, norm_g)
    proj = h @ w_in
    idx = np.cumsum(IN_SIZES)[:-1].tolist()
    (a, a_glu, z_conv, c_q, c_kv, k_rope, z_mla,
     u_sg, v_sg, z_sg) = jnp.split(proj, idx, axis=-1)

    y_conv = _conv_branch(a, a_glu, conv_w, conv_b, conv_ln_g, conv_ln_b,
                          conv_pw_w, conv_pw_b) * jax.nn.silu(z_conv)
    y_mla = _mla_branch(c_q, c_kv, k_rope, q_norm_g, w_uq, kv_norm_g, w_ukv,
                        qk_q_g, qk_k_g, cos, sin) * jax.nn.silu(z_mla)
    y_sg = _sgu_branch(u_sg, v_sg, sg_ln_g, sg_ln_b, sg_w, sg_b) * jax.nn.silu(z_sg)

    g_conv = branch_norm_g[:CONV_WIDTH]
    g_mla = branch_norm_g[CONV_WIDTH:CONV_WIDTH + MLA_WIDTH]
    g_sg = branch_norm_g[CONV_WIDTH + MLA_WIDTH:]
    y = jnp.concatenate([_rms_norm(y_conv, g_conv),
                         _rms_norm(y_mla, g_mla),
                         _rms_norm(y_sg, g_sg)], axis=-1)
    return x + y @ w_out


def setup_inputs(seed: int = 0) -> dict:
    key = jax.random.key(seed)
    ks = jax.random.split(key, 24)
    f32 = jnp.float32

    def nrm(k, shape, scale):
        return jax.random.normal(k, shape, f32) * scale

    def gain(k, shape):
        return 1.0 + 0.02 * jax.random.normal(k, shape, f32)

    L = DEPTH
    return {
        'x': jax.random.normal(ks[0], (BATCH, SEQ, D_MODEL), f32),
        'norm_g': gain(ks[1], (L, D_MODEL)),
        'w_in': nrm(ks[2], (L, D_MODEL, IN_COLS), D_MODEL ** -0.5),
        'conv_w': nrm(ks[3], (L, CONV_K, CONV_WIDTH), CONV_K ** -0.5),
        'conv_b': nrm(ks[4], (L, CONV_WIDTH), 0.01),
        'conv_ln_g': gain(ks[5], (L, CONV_WIDTH)),
        'conv_ln_b': nrm(ks[6], (L, CONV_WIDTH), 0.01),
        'conv_pw_w': nrm(ks[7], (L, CONV_WIDTH, CONV_WIDTH), CONV_WIDTH ** -0.5),
        'conv_pw_b': nrm(ks[8], (L, CONV_WIDTH), 0.01),
        'q_norm_g': gain(ks[9], (L, Q_LORA)),
        'w_uq': nrm(ks[10], (L, Q_LORA, MLA_HEADS * MLA_QK), Q_LORA ** -0.5),
        'kv_norm_g': gain(ks[11], (L, KV_LORA)),
        'w_ukv': nrm(ks[12], (L, KV_LORA, MLA_HEADS * (MLA_NOPE + MLA_V)), KV_LORA ** -0.5),
        'qk_q_g': gain(ks[13], (L, MLA_QK)),
        'qk_k_g': gain(ks[14], (L, MLA_QK)),
        'sg_ln_g': gain(ks[15], (L, SG_WIDTH)),
        'sg_ln_b': nrm(ks[16], (L, SG_WIDTH), 0.01),
        'sg_w': nrm(ks[17], (L, SG_HEADS, SG_CHUNK, SG_CHUNK), SG_CHUNK ** -0.5),
        'sg_b': 1.0 + nrm(ks[18], (L, SG_HEADS, SG_CHUNK), 0.1),
        'branch_norm_g': gain(ks[19], (L, D_MIX)),
        'w_out': nrm(ks[20], (L, D_MIX, D_MODEL), D_MIX ** -0.5),
    }


def reference(x, norm_g, w_in, conv_w, conv_b, conv_ln_g, conv_ln_b, conv_pw_w,
              conv_pw_b, q_norm_g, w_uq, kv_norm_g, w_ukv, qk_q_g, qk_k_g,
              sg_ln_g, sg_ln_b, sg_w, sg_b, branch_norm_g, w_out):
    cos, sin = _rope_tables(x.shape[1])
    for l in range(DEPTH):
        x = _layer(x, cos, sin, norm_g[l], w_in[l], conv_w[l], conv_b[l],
                   conv_ln_g[l], conv_ln_b[l], conv_pw_w[l], conv_pw_b[l],
                   q_norm_g[l], w_uq[l], kv_norm_g[l], w_ukv[l], qk_q_g[l], qk_k_g[l],
                   sg_ln_g[l], sg_ln_b[l], sg_w[l], sg_b[l], branch_norm_g[l], w_out[l])
    return x
```

```python
import os
import numpy as np
from contextlib import ExitStack
import concourse.bass as bass
import concourse.mybir as mybir
from concourse.bass_utils import run_bass_kernel_spmd

F32 = mybir.dt.float32
BF16 = mybir.dt.bfloat16
I32 = mybir.dt.int32
ALU = mybir.AluOpType
AF = mybir.ActivationFunctionType
AX = mybir.AxisListType

NCORES = 8
T = 2048
NG = 4
NT = 16
SEQ = 8192
D = 1024
INC = 3104
EPS = 1e-6
SCALE = 96 ** -0.5
NCOL = 92
NROW = 1472
SEM_EPOCH = 30000
SGU_DEPTH = 2
EPI_DEPTH = 2
P1_DEPTH = 2
EPI_SETS = 1
GELU_C = 1.5957691216057308

C_CONV = (0, 768)
C_Q = (768, 1536)
C_KV = (1536, 1824)
C_ZM = (1824, 2336)
C_SG = (2336, 3104)


class _Eng:
    def __init__(self, name):
        self.name = name
        self.ops = []
        self.seen = {}
        self.sem = None
        self.cnt = 0


class Sched:
    ENGINES = ("pe", "act", "dve", "pool", "sp")

    def __init__(self, nc, stack):
        self.nc = nc
        self.stack = stack
        self.E = {n: _Eng(n) for n in self.ENGINES}
        self.last_write = {}
        self.readers = {}
        self.dma_sems = {}
        self.all_sems = []
        self.nsem = 0
        for n in ("pe", "act", "dve", "pool"):
            self._new_epoch(self.E[n])

    def _alloc_sem(self, name):
        self.nsem += 1
        s = self.stack.enter_context(self.nc.semaphore(f"{name}_{self.nsem}"))
        return s

    def _new_epoch(self, e):
        e.sem = self._alloc_sem("c_" + e.name)
        e.cnt = 0

    def _deps(self, reads, writes):
        deps = []
        for k in reads:
            t = self.last_write.get(k)
            if t is not None:
                deps.append(t)
        for k in writes:
            t = self.last_write.get(k)
            if t is not None:
                deps.append(t)
            deps.extend(self.readers.get(k, ()))
        return deps

    def _commit(self, tok, reads, writes):
        for k in writes:
            self.last_write[k] = tok
            self.readers[k] = []
        for k in reads:
            if k not in writes:
                self.readers.setdefault(k, []).append(tok)

    def _waits(self, e, deps, skip_sem=None):
        need = {}
        for (s, v) in deps:
            if skip_sem is not None and s is skip_sem:
                continue
            if v > need.get(id(s), (None, 0))[1]:
                need[id(s)] = (s, v)
        out = []
        for sid, (s, v) in need.items():
            if v > e.seen.get(sid, 0):
                e.seen[sid] = v
                out.append((s, v))
        return out

    def op(self, eng, fn, reads=(), writes=()):
        e = self.E[eng]
        reads = list(reads)
        writes = list(writes)
        deps = self._deps(reads, writes)
        waits = self._waits(e, deps, skip_sem=e.sem if eng == "pe" else None)
        if e.cnt >= SEM_EPOCH:
            self._new_epoch(e)
        e.cnt += 1
        sem, val = e.sem, e.cnt

        def run(q, fn=fn, waits=waits, sem=sem):
            for (s, v) in waits:
                q.wait_ge(s, v)
            fn(q).then_inc(sem, 1)
        e.ops.append(run)
        tok = (sem, val)
        self._commit(tok, reads, writes)
        return tok

    def dma(self, eng, fn, semkey, reads=(), writes=()):
        e = self.E[eng]
        reads = list(reads)
        writes = list(writes)
        deps = self._deps(reads, writes)
        waits = self._waits(e, deps)
        ent = self.dma_sems.get(semkey)
        if ent is None:
            ent = [self._alloc_sem("d"), 0]
            self.dma_sems[semkey] = ent
        ent[1] += 16
        sem, val = ent[0], ent[1]

        def run(q, fn=fn, waits=waits, sem=sem):
            for (s, v) in waits:
                q.wait_ge(s, v)
            fn(q).then_inc(sem, 16)
        e.ops.append(run)
        tok = (sem, val)
        self._commit(tok, reads, writes)
        return tok

    def collective(self, eng, fn, reads=(), writes=()):
        e = self.E[eng]
        reads = list(reads)
        writes = list(writes)
        deps = self._deps(reads, writes)
        waits = self._waits(e, deps)
        sem = self._alloc_sem("cc")

        def run(q, fn=fn, waits=waits, sem=sem):
            for (s, v) in waits:
                q.wait_ge(s, v)
            fn(q).then_inc(sem)
        e.ops.append(run)
        tok = (sem, 1)
        self._commit(tok, reads, writes)
        return tok

    def wait_all(self, eng, keys):
        e = self.E[eng]
        deps = []
        for k in keys:
            if k in self.last_write:
                deps.append(self.last_write[k])
            deps.extend(self.readers.get(k, ()))
        waits = self._waits(e, deps)

        def run(q, waits=waits):
            for (s, v) in waits:
                q.wait_ge(s, v)
        e.ops.append(run)

    def barrier(self, keep=(), skip_dma=()):
        kept = {k: self.last_write[k] for k in keep if k in self.last_write}
        toks = []
        for n in ("pe", "act", "dve", "pool"):
            e = self.E[n]
            if e.cnt > 0:
                toks.append((e.sem, e.cnt))
        for k, (s, v) in self.dma_sems.items():
            if k in skip_dma:
                continue
            toks.append((s, v))
        for k, t in self.last_write.items():
            if k not in kept:
                toks.append(t)
        for n in self.ENGINES:
            e = self.E[n]
            waits = self._waits(e, toks)

            def run(q, waits=waits):
                for (s, v) in waits:
                    q.wait_ge(s, v)
            e.ops.append(run)
        self.last_write = dict(kept)
        self.readers = {}

    def custom(self, eng, run):
        self.E[eng].ops.append(run)

    def emit(self):
        nc = self.nc
        with nc.Block() as block:
            @block.tensor
            def _(q):
                for f in self.E["pe"].ops:
                    f(q)

            @block.scalar
            def _(q):
                for f in self.E["act"].ops:
                    f(q)

            @block.vector
            def _(q):
                for f in self.E["dve"].ops:
                    f(q)

            @block.gpsimd
            def _(q):
                for f in self.E["pool"].ops:
                    f(q)

            @block.sync
            def _(q):
                for f in self.E["sp"].ops:
                    f(q)


class _Stop(Exception):
    pass


def build_program(depth=2, debug=False, stop=None):
    nc = bass.Bass("TRN2", target_bir_lowering=False)
    dt = nc.dram_tensor
    xT_in = dt("xT", [D, T], F32, kind="ExternalInput").ap()
    xh_in = dt("xh", [D, 32], F32, kind="ExternalInput").ap()
    w_in = dt("w_in", [2, D, INC], F32, kind="ExternalInput").ap()
    w_uq = dt("w_uq", [2, 768, 768], F32, kind="ExternalInput").ap()
    w_ukv = dt("w_ukv", [2, 256, 1024], F32, kind="ExternalInput").ap()
    w_out = dt("w_out", [2, D, D], F32, kind="ExternalInput").ap()
    pw_in = dt("pw", [2, 256, 256], F32, kind="ExternalInput").ap()
    sgw_in = dt("sgwT", [2, 128, 512], F32, kind="ExternalInput").ap()
    colp_in = dt("colp", [2, 128, NCOL], F32, kind="ExternalInput").ap()
    rowp_in = dt("rowp", [2, NROW], F32, kind="ExternalInput").ap()
    cst_in = dt("cst", [128, 128 + 2048], F32, kind="ExternalInput").ap()
    posoff_in = dt("posoff", [128, 1], F32, kind="ExternalInput").ap()
    idx_in = dt("idx", [1, 8], I32, kind="ExternalInput").ap()
    cflag_in = dt("cflag", [128, 1], F32, kind="ExternalInput").ap()
    ofl_in = dt("ofl", [128, 2], F32, kind="ExternalInput").ap()
    outT = dt("outT", [D, T], F32, kind="ExternalOutput").ap()

    sQ = [dt(f"sQ{c}", [768, 512], BF16) for c in range(4)]
    sK = [dt(f"sK{c}", [768, 512], BF16) for c in range(4)]
    sV = [dt(f"sV{c}", [1024, 4 * 65], BF16) for c in range(4)]
    gQ = [dt(f"gQ{c}", [4 * 768, 512], BF16) for c in range(4)]
    gK = [dt(f"gK{c}", [4 * 768, 512], BF16) for c in range(4)]
    gV = [dt(f"gV{c}", [4 * 1024, 4 * 65], BF16) for c in range(4)]
    sO = [dt(f"sO{h}", [128, 4096], BF16) for h in range(2)]
    gO = [dt(f"gO{h}", [4 * 128, 4096], BF16) for h in range(2)]
    sO1a = dt("sO1a", [128, 3072], BF16)
    gO1a = dt("gO1a", [4 * 128, 3072], BF16)
    sO1b = dt("sO1b", [128, 1024], BF16)
    gO1b = dt("gO1b", [4 * 128, 1024], BF16)
    zgd = dt("zgd", [T, 512], BF16)
    xres = dt("xres", [D, T], F32)
    sTl = dt("sTl", [D, 32], F32)
    gTl = dt("gTl", [4 * D, 32], F32)
    dbg = {}

    with ExitStack() as st:
        S = Sched(nc, st)
        sb = lambda name, shape, dtype: st.enter_context(nc.sbuf_tensor(name, shape, dtype))
        ps = lambda name, shape, dtype: st.enter_context(nc.psum_tensor(name, shape, dtype))

        ident_b = sb("ident_b", [128, 128], BF16)
        ident_f = sb("ident_f", [128, 128], F32)
        masks_b = sb("masks_b", [128, 4, 512], BF16)
        cs_t = sb("cs_t", [128, NT, 32], F32)
        colp = sb("colp_t", [128, NCOL], F32)
        rowb = sb("rowb", [128, NROW], F32)
        cflag = sb("cflag_t", [128, 1], F32)
        epsc = sb("epsc", [128, 1], F32)
        negM = sb("negM", [128, 1], F32)
        onesm = sb("onesm", [128, 128], BF16)
        ones256 = sb("ones256", [128, 128], BF16)
        onescol = sb("onescol", [128, 2], BF16)
        yT = sb("yT", [128, 8, T], BF16)
        pww = sb("pww", [128, 2, 256], BF16)
        sgw = sb("sgw", [128, 4, 128], BF16)
        rstdq = sb("rstdq", [128, NT], F32)
        rstdkv = sb("rstdkv", [128, NT], F32)
        sm = sb("sm", [128, 64], F32)
        tA = sb("tA", [128, 768], F32)
        tB = sb("tB", [128, 768], F32)
        tC = sb("tC", [128, 768], F32)
        tD = sb("tD", [128, 2, 8, 16], F32)
        tE = sb("tE", [128, 2, 8, 16], F32)
        tD2 = sb("tD2", [128, 2, 8, 16], F32)
        tE2 = sb("tE2", [128, 2, 8, 16], F32)
        xfin = sb("xfin", [128, 8, 96], BF16)
        tA2 = sb("tA2", [128, 768], F32)
        tB2 = sb("tB2", [128, 768], F32)
        tC2 = sb("tC2", [128, 768], F32)
        xfin2 = sb("xfin2", [128, 8, 96], BF16)
        sm2 = sb("sm2", [128, 32], F32)
        r512a = sb("r512a", [128, 512], F32)
        r512b = sb("r512b", [128, 512], F32)
        r512c = sb("r512c", [128, 512], F32)
        r512d = sb("r512d", [128, 512], F32)
        r512e = sb("r512e", [128, 512], F32)
        r512f = sb("r512f", [128, 512], F32)
        ARENA = 116 * 512
        arena = sb("arena", [128, ARENA], BF16)

        pAB = ps("pAB", [128, 1024], F32)
        pCD = ps("pCD", [128, 1024], F32)
        pA = pAB[:, 0:512]
        pB = pAB[:, 512:1024]
        pC = pCD[:, 0:512]
        pD = pCD[:, 512:1024]
        pS = ps("pS", [128, 512], F32)
        pK = ps("pK", [128, 512], F32)
        pX = ps("pX", [128, 512], F32)
        pT = ps("pT", [128, 1024], BF16)

        class Carver:
            def __init__(self):
                self.off = 0

            def take(self, nelem_bf16):
                a = self.off
                self.off += nelem_bf16
                assert self.off <= ARENA, (self.off, ARENA)
                return arena[:, a:a + nelem_bf16]

            def bf(self, n):
                return self.take(n)

            def f32(self, n):
                return self.take(2 * n).bitcast(F32)

        pe_pend = []
        pe_depth = [0]

        def pe_flush():
            if not pe_pend:
                return
            fns = [f for (f, _, _) in pe_pend]
            rds, wrs = [], []
            for (_, r, w) in pe_pend:
                for k in r:
                    if k not in rds:
                        rds.append(k)
                for k in w:
                    if k not in wrs:
                        wrs.append(k)
            pe_pend.clear()

            def run(q, fns=fns):
                for f in fns[:-1]:
                    f(q)
                return fns[-1](q)
            S.op("pe", run, rds, wrs)

        def pe_begin():
            pe_depth[0] += 1

        def pe_end():
            pe_depth[0] -= 1
            if pe_depth[0] == 0:
                pe_flush()

        def mm(out, lhsT, rhs, start, stop, reads, writes):
            pe_pend.append((lambda q: q.matmul(out, lhsT, rhs, start=start, stop=stop), list(reads), list(writes)))
            if stop and pe_depth[0] == 0:
                pe_flush()

        def tr(out, in_, ident, reads, writes):
            pe_pend.append((lambda q: q.transpose(out, in_, ident), list(reads), list(writes)))
            if pe_depth[0] == 0:
                pe_flush()

        def act(out, in_, func, reads, writes, bias=None, scale=None):
            kw = {}
            if bias is not None:
                kw["bias"] = bias
            if scale is not None:
                kw["scale"] = scale
            S.op("act", lambda q: q.activation(out, in_, func, **kw), reads, writes)

        def tt(eng, out, in0, in1, op, reads, writes):
            S.op(eng, lambda q: q.tensor_tensor(out, in0, in1, op), reads, writes)

        def ts(eng, out, in0, s1, s2, op0, op1, reads, writes):
            if op1 is None:
                S.op(eng, lambda q: q.tensor_scalar(out, in0, s1, None, op0), reads, writes)
            else:
                S.op(eng, lambda q: q.tensor_scalar(out, in0, s1, s2, op0, op1), reads, writes)

        def stt(out, in0, scalar, in1, op0, op1, reads, writes):
            S.op("dve", lambda q: q.scalar_tensor_tensor(out, in0, scalar, in1, op0, op1), reads, writes)

        def cp(eng, out, in_, reads, writes):
            if eng == "act":
                S.op(eng, lambda q: q.activation(out, in_, AF.Identity), reads, writes)
            else:
                S.op(eng, lambda q: q.tensor_copy(out, in_), reads, writes)

        def recip(out, in_, reads, writes):
            S.op("dve", lambda q: q.reciprocal(out, in_), reads, writes)

        def red(out, in_, op, reads, writes, absv=False):
            if absv:
                S.op("dve", lambda q: q.tensor_reduce(out, in_, AX.X, op, apply_absolute_value=True), reads, writes)
            else:
                S.op("dve", lambda q: q.tensor_reduce(out, in_, AX.X, op), reads, writes)

        def rsqrt(out, in_, scale, reads, writes, tmpkey="sm_rs", tmp=None):
            if tmp is None:
                act(out, in_, AF.Sqrt, reads, writes, bias=epsc[:, 0:1], scale=scale)
                recip(out, out, writes, writes)
            else:
                act(tmp, in_, AF.Sqrt, reads, [tmpkey], bias=epsc[:, 0:1], scale=scale)
                recip(out, tmp, [tmpkey], writes)

        def dma(eng, out, in_, semkey, reads, writes):
            S.dma(eng, lambda q: q.dma_start(out=out, in_=in_), semkey, reads, writes)

        def pipe(gens, depth):
            gens = iter(gens)
            active = []
            done = False
            while True:
                while not done and len(active) < depth:
                    try:
                        nx = next(gens)
                    except StopIteration:
                        done = True
                        break
                    if nx is None:
                        for gn in active:
                            for _ in gn:
                                pass
                        active = []
                        continue
                    active.append(nx)
                if not active:
                    break
                for gn in list(active):
                    try:
                        next(gn)
                    except StopIteration:
                        active.remove(gn)


        regs = {}
        offs = {}

        IDX_NAMES = ("o192", "o256", "o2048", "oT", "o1024", "oA")
        QREGS = {"sp": ("o192", "o256", "oT"), "act": ("o192", "o1024", "oA"), "pool": ("o256",)}

        def mk_load_regs(qn):
            def load_regs(q):
                for nm in QREGS[qn]:
                    i = IDX_NAMES.index(nm)
                    r = st.enter_context(q.register("r_" + qn + nm))
                    q.reg_load(r, idx_in[0:1, i:i + 1])
                    offs[(qn, nm)] = q.snap(r)
            return load_regs
        for qn in ("sp", "act", "pool"):
            S.custom(qn, mk_load_regs(qn))

        dma("pool", ident_b[:], cst_in[:, 0:128], "c0", [], ["ident_b"])
        dma("pool", masks_b[:].rearrange("p a b -> p (a b)"), cst_in[:, 128:2176], "c1", [], ["masks_b"])
        dma("sp", ident_f[:], cst_in[:, 0:128], "c2", [], ["ident_f"])
        import math
        posoff = sb("posoff_t", [128, 1], F32)
        dma("sp", posoff[:], posoff_in[:, :], "c3", [], ["posoff"])
        pos_i = r512a[:, 256:256 + NT].bitcast(I32)
        pos_f = r512a[:, 288:288 + NT]
        invf = r512a[:, 320:336]
        ang = r512a[:, 0:256].rearrange("p (t i) -> p t i", i=16)
        kq_i = r512b[:, 0:256].bitcast(I32).rearrange("p (t i) -> p t i", i=16)
        kq_f = r512b[:, 256:512].rearrange("p (t i) -> p t i", i=16)
        red_t = r512c[:, 0:256].rearrange("p (t i) -> p t i", i=16)
        arg_t = r512c[:, 256:512].rearrange("p (t i) -> p t i", i=16)
        S.op("pool", lambda q: q.iota(pos_i, [[128, NT]], base=0, channel_multiplier=1), [], ["pos_i"])
        cp("dve", pos_f[:], pos_i[:], ["pos_i"], ["pos_f"])
        ts("dve", pos_f[:], pos_f[:], posoff[:, 0:1], None, ALU.add, None, ["pos_f", "posoff"], ["pos_f"])
        inv_freq = (np.float32(10000.0) ** (-np.arange(16, dtype=np.float32) / np.float32(16))).astype(np.float32)
        for i in range(16):
            S.op("dve", lambda q, i=i: q.memset(invf[:, i:i + 1], float(inv_freq[i])), [], ["invf"])
        tt("dve", ang[:], pos_f[:].unsqueeze(2).to_broadcast([128, NT, 16]),
           invf[:].unsqueeze(1).to_broadcast([128, NT, 16]), ALU.mult, ["pos_f", "invf"], ["ang"])
        ts("dve", kq_f[:], ang[:], 1.0 / (2.0 * math.pi), None, ALU.mult, None, ["ang"], ["kq_f"])
        cp("dve", kq_i[:], kq_f[:], ["kq_f"], ["kq_i"])
        cp("dve", kq_f[:], kq_i[:], ["kq_i"], ["kq_f"])
        C1 = 6.28125
        C2 = float(np.float32(2.0 * math.pi - C1))
        C3 = float(2.0 * math.pi - C1 - C2)
        stt(red_t[:], kq_f[:], -C1, ang[:], ALU.mult, ALU.add, ["ang", "kq_f"], ["red_t"])
        stt(red_t[:], kq_f[:], -C2, red_t[:], ALU.mult, ALU.add, ["red_t", "kq_f"], ["red_t"])
        stt(red_t[:], kq_f[:], -C3, red_t[:], ALU.mult, ALU.add, ["red_t", "kq_f"], ["red_t"])
        PI_LO = 3.1415925
        TWO_PI = 2.0 * math.pi
        wrp = r512d[:, 0:256].rearrange("p (t i) -> p t i", i=16)
        for shift, lo in ((math.pi / 2.0, 0), (0.0, 16)):
            ts("dve", arg_t[:], red_t[:], shift, None, ALU.add, None, ["red_t"], ["arg_t"])
            for _ in range(2):
                ts("dve", wrp[:], arg_t[:], math.pi, TWO_PI, ALU.is_gt, ALU.mult, ["arg_t"], ["wrp"])
                tt("dve", arg_t[:], arg_t[:], wrp[:], ALU.subtract, ["arg_t", "wrp"], ["arg_t"])
                ts("dve", wrp[:], arg_t[:], -math.pi, TWO_PI, ALU.is_lt, ALU.mult, ["arg_t"], ["wrp"])
                tt("dve", arg_t[:], arg_t[:], wrp[:], ALU.add, ["arg_t", "wrp"], ["arg_t"])
            ts("dve", arg_t[:], arg_t[:], PI_LO, -PI_LO, ALU.min, ALU.max, ["arg_t"], ["arg_t"])
            act(cs_t[:, :, lo:lo + 16], arg_t[:], AF.Sin, ["arg_t"], ["cs_t"])
        dma("sp", cflag[:], cflag_in[:, :], "c4", [], ["cflag"])
        ofl = sb("ofl_t", [128, 2], F32)
        dma("sp", ofl[:], ofl_in[:, :], "c5", [], ["ofl"])
        S.op("dve", lambda q: q.memset(epsc[:], EPS), [], ["epsc"])
        S.op("dve", lambda q: q.memset(onesm[:], 1.0 / 1024), [], ["onesm"])
        S.op("dve", lambda q: q.memset(ones256[:], 1.0 / 256), [], ["ones256"])
        S.op("dve", lambda q: q.memset(onescol[:, 0:1], 1.0 / 256), [], ["onescol"])
        S.op("dve", lambda q: q.memset(onescol[:, 1:2], 1.0 / 768), [], ["onescol"])

        def dump(name, ap_sb, shape, dtype, reads):
            if not debug:
                return
            d = dt("dbg_" + name, shape, dtype, kind="ExternalOutput").ap()
            dbg[name] = d
            dma("sp", d, ap_sb, "dbg_" + name, reads, ["dbg_" + name])

        dump("cs", cs_t[:], [128, NT, 32], F32, ["cs_t"])

        def chk(level, l):
            if stop is not None and stop == (l, level):
                raise _Stop()

        try:
          for l in range(depth):
              x_src = xT_in if l == 0 else xres.ap()
              x_dst = outT if l == depth - 1 else xres.ap()
              xs3 = x_src.rearrange("(k p) t -> p k t", p=128)
              xd3 = x_dst.rearrange("(k p) t -> p k t", p=128)
              L = f"L{l}"

              S.barrier(keep=("gTl",))
              cv = Carver()
              hT = cv.bf(8 * 2080).rearrange("p (k t) -> p k t", k=8)
              wbufA = cv.bf(8 * 768).rearrange("p (k c) -> p k c", k=8)
              wbufB = cv.bf(8 * 768).rearrange("p (k c) -> p k c", k=8)
              wuq = cv.bf(6 * 768).rearrange("p (k c) -> p k c", k=6)
              wukv = cv.bf(2 * 1024).rearrange("p (k c) -> p k c", k=2)
              shared0 = cv.off
              xgb = [cv.f32(8 * 512).rearrange("p (k t) -> p k t", k=8) for _ in range(2)]
              sq8 = cv.bf(8 * 512).rearrange("p (k t) -> p k t", k=8)
              assert cv.off <= ARENA
              cv.off = shared0
              stgQ = cv.bf(8 * 512).rearrange("p (h t) -> p h t", h=8)
              stgK = cv.bf(8 * 512).rearrange("p (h t) -> p h t", h=8)
              vstb = [cv.bf(8 * 4 * 65).rearrange("p (h t c) -> p h t c", h=8, t=4) for _ in range(2)]
              cqTb = [cv.bf(6 * 512).rearrange("p (k t) -> p k t", k=6) for _ in range(2)]
              ckvTb = [cv.bf(2 * 512).rearrange("p (k t) -> p k t", k=2) for _ in range(2)]
              csq = cv.bf(6 * 512).rearrange("p (k t) -> p k t", k=6)
              endA = cv.off

              dma("sp", colp[:], colp_in[l], "colp", [], ["colp"])
              dma("sp", rowb[:], rowp_in[l:l + 1, :].to_broadcast([128, NROW]), "rowb", [], ["rowb"])
              win3 = w_in[l].rearrange("(k p) c -> p k c", p=128)
              dma("pool", wbufA[:, :, 0:768], win3[:, :, C_Q[0]:C_Q[1]], "wbufA", [], ["wbufA"])
              dma("pool", wbufB[:, :, 0:288], win3[:, :, C_KV[0]:C_KV[1]], "wbufB", [], ["wbufB"])
              dma("pool", wuq[:], w_uq[l].rearrange("(k p) c -> p k c", p=128), "wuq", [], ["wuq"])
              dma("pool", wukv[:], w_ukv[l].rearrange("(k p) c -> p k c", p=128), "wukv", [], ["wukv"])
              dma("pool", pww[:], pw_in[l].rearrange("(k p) c -> p k c", p=128), "pww", [], ["pww"])
              dma("pool", sgw[:].rearrange("p g t -> p (g t)"), sgw_in[l], "sgw", [], ["sgw"])
              tt("dve", sgw[:], sgw[:], masks_b[:, 0, 0:128].unsqueeze(1).to_broadcast([128, 4, 128]),
                 ALU.mult, ["sgw", "masks_b"], ["sgw"])
              red(sm[:, 60:61], rowb[:, 0:96], ALU.max, ["rowb"], ["sm60"], absv=True)
              red(sm[:, 61:62], rowb[:, 96:192], ALU.max, ["rowb"], ["sm61"], absv=True)
              tt("dve", sm[:, 62:63], sm[:, 60:61], sm[:, 61:62], ALU.mult, ["sm60", "sm61"], ["sm62"])
              ts("dve", negM[:], sm[:, 62:63], -96.0 * SCALE, None, ALU.mult, None, ["sm62"], ["negM"])

              P1S = [dict(xg=xgb[0], kxg="xg0", rs=r512a, krs="r512a", ps=pS, kps="pS"),
                     dict(xg=xgb[1], kxg="xg1", rs=r512d, krs="r512d", ps=pK, kps="pK")]

              def p1_chain(idx, load_fn, n, hcols, key, halo=False):
                  P = P1S[idx % 2]
                  xgx, kxg, rs, krs, psx, kps = (P[k] for k in ("xg", "kxg", "rs", "krs", "ps", "kps"))
                  S.dma("sp", lambda q: load_fn(q, xgx[:, :, 0:n]), kxg, ["gTl"] if (halo and l > 0) else [], [kxg])
                  yield
                  if halo:
                      ts("dve", xgx[:, :, 0:n], xgx[:, :, 0:n], cflag[:, 0:1], None, ALU.mult, None, [kxg, "cflag"], [kxg])
                  act(sq8[:, :, 0:n], xgx[:, :, 0:n], AF.Square, [kxg], ["sq8"])
                  for kc in range(8):
                      mm(psx[:, 0:n], onesm[:], sq8[:, kc, 0:n], kc == 0, kc == 7, ["sq8", "onesm"], [kps])
                  yield
                  act(rs[:, 0:n], psx[:, 0:n], AF.Sqrt, [kps, "epsc"], [krs], bias=epsc[:, 0:1], scale=1.0)
                  yield
                  recip(rs[:, 0:n], rs[:, 0:n], [krs], [krs])
                  yield
                  for kc in range(8):
                      stt(hT[:, kc, hcols[0]:hcols[1]], xgx[:, kc, 0:n], colp[:, kc:kc + 1], rs[:, 0:n],
                          ALU.mult, ALU.mult, [kxg, "colp", krs], [key])
                      if kc % 2 == 1:
                          yield

              def p1_all():
                  for g in range(NG):
                      src = xs3[:, :, g * 512:(g + 1) * 512]
                      yield p1_chain(g, lambda q, o, src=src: q.dma_start(out=o, in_=src), 512,
                                     (g * 512, (g + 1) * 512), f"hT{g}")
                  if l == 0:
                      hsrc = xh_in.rearrange("(k p) t -> p k t", p=128)
                      yield p1_chain(NG, lambda q, o: q.dma_start(out=o, in_=hsrc), 32, (2048, 2080), "hTh", halo=True)
                  else:
                      yield p1_chain(NG, lambda q, o: q.dma_start(
                          out=o, in_=gTl.ap()[bass.ds(offs[("sp", "oT")], D), :].rearrange("(k p) t -> p k t", p=128)),
                          32, (2048, 2080), "hTh", halo=True)

              pipe(p1_all(), P1_DEPTH)
              if l == 0:
                  dump("hT", hT[:, :, 0:2048], [128, 8, 2048], BF16, [f"hT{g}" for g in range(NG)])

              chk(1, l)
              S.barrier(keep=("wuq", "wukv"), skip_dma=("wuq", "wukv"))
              for vb in range(2):
                  S.op("dve", lambda q, vb=vb: q.memset(vstb[vb][:, :, :, 64:65], 1.0), [], [f"vst{vb}"])
              RG = [[0, 1, 2, 3], [4, 5, 6, 7]]
              GK = tuple(f"g{n}{c}" for n in "QKV" for c in range(4))

              def rr(*gens):
                  gens = list(gens)
                  while gens:
                      for gn in list(gens):
                          try:
                              next(gn)
                          except StopIteration:
                              gens.remove(gn)

              PQ = dict(n="q", tA=tA, tB=tB, tC=tC, tD=tD, tE=tE, xf=xfin, sm=sm, pT=pT, pTk="pT")
              PK = dict(n="k", tA=tA2, tB=tB2, tC=tC2, tD=tD2, tE=tE2, xf=xfin2, sm=sm2,
                        pT=pS[:, :].bitcast(BF16), pTk="pS")

              def norm_rope_gen(P, gain_lo, tile_i):
                  n = P["n"]
                  kA, kB, kC, kD, kE = (f"tA{n}", f"tB{n}", f"tC{n}", f"tD{n}", f"tE{n}")
                  smx = P["sm"]
                  X3 = P["tA"][:].rearrange("p (h d) -> p h d", h=8)
                  sq3 = P["tB"][:].rearrange("p (h d) -> p h d", h=8)
                  Xn = P["tC"][:].rearrange("p (h d) -> p h d", h=8)
                  xf = P["xf"]
                  tDn, tEn = P["tD"], P["tE"]
                  tt("pool" if n == "k" else "dve", sq3, X3, X3, ALU.mult, [kA], [kB])
                  yield
                  red(smx[:, 0:8], sq3, ALU.add, [kB], [f"sm0{n}"])
                  yield
                  act(smx[:, 16:24], smx[:, 0:8], AF.Sqrt, [f"sm0{n}", "epsc"], [f"sm16{n}"], bias=epsc[:, 0:1],
                      scale=1.0 / 96)
                  yield
                  recip(smx[:, 8:16], smx[:, 16:24], [f"sm16{n}"], [f"sm8{n}"])
                  yield
                  tt("dve", Xn, X3, smx[:, 8:16].unsqueeze(2).to_broadcast([128, 8, 96]), ALU.mult, [kA, f"sm8{n}"], [kC])
                  yield
                  tt("pool", Xn, Xn, rowb[:, gain_lo:gain_lo + 96].unsqueeze(1).to_broadcast([128, 8, 96]), ALU.mult,
                     [kC, "rowb"], [kC])
                  yield
                  cosb = cs_t[:, tile_i, 0:16].unsqueeze(1).to_broadcast([128, 8, 16])
                  sinb = cs_t[:, tile_i, 16:32].unsqueeze(1).to_broadcast([128, 8, 16])
                  x1 = Xn[:, :, 64:80]
                  x2 = Xn[:, :, 80:96]
                  cp("act", xf[:, :, 0:64], Xn[:, :, 0:64], [kC], [f"xf0{n}"])
                  tt("dve", tDn[:, 0], x1, cosb, ALU.mult, [kC, "cs_t"], [kD + "0"])
                  tt("pool", tEn[:, 0], x2, sinb, ALU.mult, [kC, "cs_t"], [kE + "0"])
                  yield
                  tt("dve", tDn[:, 1], x1, sinb, ALU.mult, [kC, "cs_t"], [kD + "1"])
                  tt("pool", tEn[:, 1], x2, cosb, ALU.mult, [kC, "cs_t"], [kE + "1"])
                  yield
                  tt("dve", xf[:, :, 64:80], tDn[:, 0], tEn[:, 0], ALU.subtract, [kD + "0", kE + "0"], [f"xf1{n}"])
                  yield
                  tt("dve", xf[:, :, 80:96], tDn[:, 1], tEn[:, 1], ALU.add, [kD + "1", kE + "1"], [f"xf2{n}"])
                  yield

              def transpose_gen(P, stg, i, stgkey):
                  n = P["n"]
                  xf = P["xf"]
                  pe_begin()
                  for h in range(8):
                      tr(P["pT"][0:96, h * 128:(h + 1) * 128], xf[:, h, :], ident_b[:],
                         [f"xf0{n}", f"xf1{n}", f"xf2{n}", "ident_b"], [P["pTk"]])
                  pe_end()
                  yield
                  cp("act", stg[0:96, :, i * 128:(i + 1) * 128],
                     P["pT"][0:96, :].rearrange("p (h t) -> p h t", h=8), [P["pTk"]], [stgkey])
                  yield

              def chain_q(g, i):
                  ti = g * 4 + i
                  tcs = slice(i * 128, (i + 1) * 128)
                  for ct in range(6):
                      mm(pA[:], cqTb[g % 2][:, ct, tcs], wuq[:, ct, 0:512], ct == 0, ct == 5, [f"cqT{g % 2}", "wuq"], ["pA"])
                  for ct in range(6):
                      mm(pB[:, 0:256], cqTb[g % 2][:, ct, tcs], wuq[:, ct, 512:768], ct == 0, ct == 5, [f"cqT{g % 2}", "wuq"], ["pB"])
                  yield
                  ts("dve", tA[:, 0:512], pA[:], rstdq[:, ti:ti + 1], None, ALU.mult, None, ["pA", f"rstdq{g}"], ["tAq"])
                  act(tA[:, 512:768], pB[:, 0:256], AF.Identity, ["pB", f"rstdq{g}"], ["tAq"], scale=rstdq[:, ti:ti + 1])
                  yield
                  yield from norm_rope_gen(PQ, 0, ti)
                  yield from transpose_gen(PQ, stgQ, i, "stgQ")

              def chain_k(g, i):
                  ti = g * 4 + i
                  tcs = slice(i * 128, (i + 1) * 128)
                  hk = f"hT{g}"
                  for ct in range(2):
                      mm(pC[:], ckvTb[g % 2][:, ct, tcs], wukv[:, ct, 0:512], ct == 0, ct == 1, [f"ckvT{g % 2}", "wukv"], ["pC"])
                  for ct in range(2):
                      mm(pD[:], ckvTb[g % 2][:, ct, tcs], wukv[:, ct, 512:1024], ct == 0, ct == 1, [f"ckvT{g % 2}", "wukv"], ["pD"])
                  for kc in range(8):
                      mm(pK[:, 0:32], hT[:, kc, g * 512 + i * 128:g * 512 + (i + 1) * 128], wbufB[:, kc, 256:288],
                         kc == 0, kc == 7, [hk, "wbufB"], ["pK"])
                  yield
                  kf = tA2[:].rearrange("p (h d) -> p h d", h=8)
                  pvC = pC[:].rearrange("p (h c) -> p h c", h=4)
                  pvD = pD[:].rearrange("p (h c) -> p h c", h=4)
                  rk = rstdkv[:, ti:ti + 1]
                  ts("dve", kf[:, 0:4, 0:64], pvC[:, :, 0:64], rk, None, ALU.mult, None, ["pC", f"rstdkv{g}"], ["tAk"])
                  act(kf[:, 4:8, 0:64], pvD[:, :, 0:64], AF.Identity, ["pD", f"rstdkv{g}"], ["tAk"], scale=rk)
                  yield
                  cp("dve", kf[:, :, 64:96], pK[:, 0:32].unsqueeze(1).to_broadcast([128, 8, 32]), ["pK"], ["tAk"])
                  vst = vstb[g % 2]
                  kvst = f"vst{g % 2}"
                  act(vst[:, 0:4, i, 0:64], pvC[:, :, 64:128], AF.Identity, ["pC", f"rstdkv{g}"], [kvst], scale=rk)
                  yield
                  ts("dve", vst[:, 4:8, i, 0:64], pvD[:, :, 64:128], rk, None, ALU.mult, None, ["pD", f"rstdkv{g}"], [kvst])
                  yield
                  yield from norm_rope_gen(PK, 96, ti)
                  yield from transpose_gen(PK, stgK, i, "stgK")

              def gl_chain(g):
                  gc = slice(g * 512, (g + 1) * 512)
                  hk = f"hT{g}"
                  cq, kcq, ckv, kckv = cqTb[g % 2], f"cqT{g % 2}", ckvTb[g % 2], f"ckvT{g % 2}"
                  for ct in range(6):
                      for kc in range(8):
                          mm(pX[:], wbufA[:, kc, ct * 128:(ct + 1) * 128], hT[:, kc, gc], kc == 0, kc == 7,
                             ["wbufA", hk], ["pX"])
                      act(cq[:, ct, :], pX[:], AF.Identity, ["pX", "colp"], [kcq], scale=colp[:, 16 + ct:17 + ct])
                      act(csq[:, ct, :], pX[:], AF.Square, ["pX"], ["csq"])
                      yield
                  for i in range(4):
                      for ct in range(6):
                          mm(pX[:, i:i + 1], csq[:, ct, i * 128:(i + 1) * 128], onescol[:, 1:2], ct == 0, ct == 5,
                             ["csq", "onescol"], ["pX"])
                  rsqrt(rstdq[:, g * 4:(g + 1) * 4], pX[:, 0:4], 1.0, ["pX", "epsc"], [f"rstdq{g}"],
                        tmp=sm[:, 24:28], tmpkey="sm24")
                  yield
                  for ct in range(2):
                      for kc in range(8):
                          mm(pX[:], wbufB[:, kc, ct * 128:(ct + 1) * 128], hT[:, kc, gc], kc == 0, kc == 7,
                             ["wbufB", hk], ["pX"])
                      act(ckv[:, ct, :], pX[:], AF.Identity, ["pX", "colp"], [kckv], scale=colp[:, 22 + ct:23 + ct])
                      act(csq[:, ct, :], pX[:], AF.Square, ["pX"], ["csq"])
                      yield
                  for i in range(4):
                      for ct in range(2):
                          mm(pX[:, i:i + 1], csq[:, ct, i * 128:(i + 1) * 128], onescol[:, 0:1], ct == 0, ct == 1,
                             ["csq", "onescol"], ["pX"])
                  rsqrt(rstdkv[:, g * 4:(g + 1) * 4], pX[:, 0:4], 1.0, ["pX", "epsc"], [f"rstdkv{g}"],
                        tmp=sm[:, 28:32], tmpkey="sm28")
                  yield

              def tiles(g):
                  for i in range(4):
                      act2 = [chain_q(g, i), chain_k(g, i)]
                      while act2:
                          for gn in list(act2):
                              try:
                                  next(gn)
                              except StopIteration:
                                  act2.remove(gn)
                          yield
                  dma("sp", sQ[g].ap().rearrange("(h d) t -> d h t", d=96), stgQ[0:96, :, :], "stgQ", ["stgQ"], [f"sQ{g}"])
                  dma("sp", sK[g].ap().rearrange("(h d) t -> d h t", d=96), stgK[0:96, :, :], "stgK", ["stgK"], [f"sK{g}"])
                  dma("sp", sV[g].ap().rearrange("(h p) (t c) -> p h t c", p=128, c=65), vstb[g % 2][:, :, :, :],
                      f"vst{g % 2}", [f"vst{g % 2}"], [f"sV{g}"])
                  if not (stop is not None and stop == (l, 2)):
                      for sname, s_t, g_t in (("Q", sQ[g], gQ[g]), ("K", sK[g], gK[g]), ("V", sV[g], gV[g])):
                          S.collective("pool", lambda q, s_t=s_t, g_t=g_t: q.collective_compute(
                              "AllGather", ALU.bypass, replica_groups=RG, ins=[s_t.ap().opt()], outs=[g_t.ap().opt()]),
                              [f"s{sname}{g}"], [f"g{sname}{g}"])
                  yield

              for _ in gl_chain(0):
                  pass
              for g in range(NG):
                  if g + 1 < NG:
                      rr(tiles(g), gl_chain(g + 1))
                  else:
                      rr(tiles(g))
                  if g == NG - 2:
                      dma("pool", wbufA[:, :, 0:768], win3[:, :, C_CONV[0]:C_CONV[1]], "wbufA", [], ["wbufA"])

              chk(2, l)
              chk(3, l)
              S.barrier(keep=GK)
              cv.off = shared0
              ybuf = cv.bf(2 * 2080).rearrange("p (c t) -> p c t", c=2)
              zc = cv.bf(2 * 2048).rearrange("p (c t) -> p c t", c=2)
              diag = cv.bf(62 * 128).rearrange("p (j m) -> p j m", j=62)
              c32 = cv.f32(2 * 512).rearrange("p (c t) -> p c t", c=2)
              cbf = cv.bf(2 * 512).rearrange("p (c t) -> p c t", c=2)
              csq2 = cv.bf(2 * 512).rearrange("p (c t) -> p c t", c=2)
              actv = cv.bf(2 * 512).rearrange("p (c t) -> p c t", c=2)
              yc32 = cv.f32(2 * 512).rearrange("p (c t) -> p c t", c=2)

              for ct in range(2):
                  for j in range(31):
                      col = 30 + ct * 31 + j
                      ts("dve", diag[:, ct * 31 + j, :], ident_b[:], colp[:, col:col + 1], None, ALU.mult, None,
                         ["ident_b", "colp"], ["diag"])
              spans = [("hTh", slice(2048, 2080), slice(0, 32), 32, None)]
              for g in range(NG):
                  spans.append((f"hT{g}", slice(g * 512, (g + 1) * 512), slice(32 + g * 512, 32 + (g + 1) * 512), 512, g))
              for hk, hcols, ycols, n, g in spans:
                  for ct in range(2):
                      for kc in range(8):
                          mm(pA[:, 0:n], wbufA[:, kc, ct * 128:(ct + 1) * 128], hT[:, kc, hcols], kc == 0, kc == 7,
                             ["wbufA", hk], ["pA"])
                      for kc in range(8):
                          mm(pB[:, 0:n], wbufA[:, kc, 256 + ct * 128:256 + (ct + 1) * 128], hT[:, kc, hcols], kc == 0,
                             kc == 7, ["wbufA", hk], ["pB"])
                      act(r512b[:, 0:n], pB[:, 0:n], AF.Sigmoid, ["pB"], ["r512b"])
                      tt("dve", ybuf[:, ct, ycols], pA[:, 0:n], r512b[:, 0:n], ALU.mult, ["pA", "r512b"], ["ybuf"])
                      if g is not None:
                          for kc in range(8):
                              mm(pC[:], wbufA[:, kc, 512 + ct * 128:512 + (ct + 1) * 128], hT[:, kc, hcols], kc == 0,
                                 kc == 7, ["wbufA", hk], ["pC"])
                          act(zc[:, ct, hcols], pC[:], AF.Silu, ["pC"], ["zc"])
              dma("pool", wbufB[:, :, 0:768], win3[:, :, C_SG[0]:C_SG[1]], "wbufB", [], ["wbufB"])
              dma("pool", wbufA[:, :, 0:512], win3[:, :, C_ZM[0]:C_ZM[1]], "wbufA", [], ["wbufA"])
              for g in range(NG):
                  gc = slice(g * 512, (g + 1) * 512)
                  for ct in range(2):
                      for j in range(31):
                          o = 32 + g * 512 - 30 + j
                          mm(pX[:], diag[:, ct * 31 + j, :], ybuf[:, ct, o:o + 512], j == 0, j == 30,
                             ["diag", "ybuf"], ["pX"])
                      act(c32[:, ct, :], pX[:], AF.Identity, ["pX", "colp"], ["c32"], bias=colp[:, 8 + ct:9 + ct])
                      act(csq2[:, ct, :], pX[:], AF.Square, ["pX", "colp"], ["csq2"], bias=colp[:, 8 + ct:9 + ct])
                      cp("dve", cbf[:, ct, :], c32[:, ct, :], ["c32"], ["cbf"])
                  for ct in range(2):
                      mm(pS[:], ones256[:], cbf[:, ct, :], ct == 0, ct == 1, ["ones256", "cbf"], ["pS"])
                  for ct in range(2):
                      mm(pK[:], ones256[:], csq2[:, ct, :], ct == 0, ct == 1, ["ones256", "csq2"], ["pK"])
                  cp("act", r512a[:], pS[:], ["pS"], ["r512a"])
                  tt("dve", r512b[:], r512a[:], r512a[:], ALU.mult, ["r512a"], ["r512b"])
                  tt("dve", r512b[:], pK[:], r512b[:], ALU.subtract, ["pK", "r512b"], ["r512b"])
                  ts("dve", r512b[:], r512b[:], 0.0, EPS, ALU.max, ALU.add, ["r512b"], ["r512b"])
                  act(r512c[:], r512b[:], AF.Sqrt, ["r512b"], ["r512c"])
                  recip(r512c[:], r512c[:], ["r512c"], ["r512c"])
                  for ct in range(2):
                      tt("dve", c32[:, ct, :], c32[:, ct, :], r512a[:], ALU.subtract, ["c32", "r512a"], ["c32"])
                      tt("dve", c32[:, ct, :], c32[:, ct, :], r512c[:], ALU.mult, ["c32", "r512c"], ["c32"])
                      act(actv[:, ct, :], c32[:, ct, :], AF.Silu, ["c32", "colp"], ["actv"],
                          bias=colp[:, 12 + ct:13 + ct], scale=colp[:, 10 + ct:11 + ct])
                  for co in range(2):
                      for ci in range(2):
                          mm(pA[:], pww[:, ci, co * 128:(co + 1) * 128], actv[:, ci, :], ci == 0, ci == 1,
                             ["pww", "actv"], ["pA"])
                      stt(yc32[:, co, :], pA[:], colp[:, 14 + co:15 + co], zc[:, co, gc], ALU.add, ALU.mult,
                          ["pA", "colp", "zc"], ["yc32"])
                      act(csq2[:, co, :], yc32[:, co, :], AF.Square, ["yc32"], ["csq2"])
                  for co in range(2):
                      mm(pS[:], ones256[:], csq2[:, co, :], co == 0, co == 1, ["ones256", "csq2"], ["pS"])
                  rsqrt(r512a[:], pS[:], 1.0, ["pS", "epsc"], ["r512a"])
                  for co in range(2):
                      stt(yT[:, co, gc], yc32[:, co, :], colp[:, 24 + co:25 + co], r512a[:], ALU.mult, ALU.mult,
                          ["yc32", "colp", "r512a"], [f"yT{g}"])

              chk(4, l)
              S.barrier(keep=GK)
              cvs = Carver()
              cvs.off = shared0
              QT = cvs.bf(SEQ)
              KT0 = cvs.bf(SEQ)
              Vstg = cvs.bf(64 * 65).rearrange("p (t c) -> p t c", c=65)
              ptb = [cvs.bf(1024) for _ in range(3)]
              KTb = [KT0, None]
              QTb = [QT, None]
              def load_q(hh):
                  for c in range(4):
                      S.dma("sp", lambda q, hh=hh, c=c: q.dma_start(
                          out=QTb[hh][0:96, :].rearrange("p (s c t) -> p s c t", s=4, c=4)[:, :, c, :],
                          in_=gQ[c].ap().rearrange("(s r) t -> r s t", s=4)[hh * 96:768, :, :][
                              bass.ds(offs[("sp", "o192")], 96), :, :]),
                          "QT" if hh == 0 else "QT1", [f"gQ{c}"], ["QT" if hh == 0 else "QT1"])

              def load_k(hh, eng):
                  for c in range(4):
                      S.dma(eng, lambda q, hh=hh, c=c: q.dma_start(
                          out=KTb[hh][0:96, :].rearrange("p (s c t) -> p s c t", s=4, c=4)[:, :, c, :],
                          in_=gK[c].ap().rearrange("(s r) t -> r s t", s=4)[hh * 96:768, :, :][
                              bass.ds(offs[(eng, "o192")], 96), :, :]),
                          f"KT{hh}", [f"gK{c}"], [f"KT{hh}"])

              def load_v(hh, eng):
                  for c in range(4):
                      S.dma(eng, lambda q, hh=hh, c=c: q.dma_start(
                          out=Vstg[:, :, :].rearrange("p (s c t) x -> p s c (t x)", s=4, c=4)[:, :, c, :],
                          in_=gV[c].ap().rearrange("(s r) x -> r s x", s=4)[hh * 128:1024, :, :][
                              bass.ds(offs[(eng, "o256")], 128), :, :]),
                          "Vstg", [f"gV{c}"], ["Vstg"])

              load_k(0, "sp")
              load_v(0, "sp")
              load_q(0)

              xk = ["xfin0", "xfin1", "xfin2"]

              SG = [
                  dict(n="0", pUV=pA, kUV="pA", pZM=pB, kZM="pB", pTT=pT, kTT="pT", pG=pX, kG="pX",
                       ra=r512a, rb=r512b, rc=r512c, tB=tB, tC=tC, xf=xfin, sm=sm[:, 32:48], zst=tA[:, 0:256].bitcast(BF16)),
                  dict(n="1", pUV=pC, kUV="pC", pZM=pD, kZM="pD", pTT=pK[:, :].bitcast(BF16), kTT="pK", pG=pS, kG="pS",
                       ra=r512d, rb=r512e, rc=r512f, tB=tB2, tC=tC2, xf=xfin2, sm=sm2[:, 0:16],
                       zst=tA2[:, 0:256].bitcast(BF16)),
              ]

              def sgu_chain(ti):
                  P = SG[ti % 2]
                  n = P["n"]
                  g = ti // 4
                  tc = slice(ti * 128, (ti + 1) * 128)
                  hk = f"hT{g}"
                  pUV, kUV, pZM, kZM, pTT, kTT, pG, kG = (P[k] for k in ("pUV", "kUV", "pZM", "kZM", "pTT", "kTT", "pG", "kG"))
                  ra, rb, rc, tBn, tCn, smx = P["ra"], P["rb"], P["rc"], P["tB"], P["tC"], P["sm"]
                  kra, krb, krc = ("r512a", "r512b", "r512c") if n == "0" else ("r512d", "r512e", "r512f")
                  ktB, ktC = f"tB{n}", f"tC{n}"
                  xf2 = P["xf"][:].rearrange("p h d -> p (h d)")
                  vln = xf2[:, 0:256]
                  ysg = xf2[:, 256:512]
                  kx = f"xf{n}"
                  for kc in range(8):
                      mm(pUV[:, :], hT[:, kc, tc], wbufB[:, kc, 0:512], kc == 0, kc == 7, [hk, "wbufB"], [kUV])
                  for kc in range(8):
                      mm(pZM[:, 0:256], hT[:, kc, tc], wbufB[:, kc, 512:768], kc == 0, kc == 7, [hk, "wbufB"], [kZM])
                  for kc in range(8):
                      mm(pG[:, :], hT[:, kc, tc], wbufA[:, kc, 0:512], kc == 0, kc == 7, [hk, "wbufA"], [kG])
                  yield
                  act(ra[:], pUV[:, :], AF.Square, [kUV], [kra])
                  yield
                  ts("dve", ra[:], ra[:], 0.044715, 1.0, ALU.mult, ALU.add, [kra], [kra])
                  yield
                  tt("dve", ra[:], ra[:], pUV[:, :], ALU.mult, [kra, kUV], [kra])
                  yield
                  act(ra[:], ra[:], AF.Sigmoid, [kra], [kra], scale=GELU_C)
                  yield
                  tt("dve", rb[:], ra[:], pUV[:, :], ALU.mult, [kra, kUV], [krb])
                  act(rc[:, 0:256], pZM[:, 0:256], AF.Silu, [kZM], [krc])
                  yield
                  act(P["zst"], pG[:, :], AF.Silu, [kG], [f"zst{n}"])
                  dma("sp", zgd.ap()[ti * 128:(ti + 1) * 128, :], P["zst"], f"zst{n}", [f"zst{n}"], ["zgd"])
                  S.op("dve", lambda q: q.bn_stats(smx[:, 0:6], rb[:, 256:512]), [krb], [f"sgs0{n}"])
                  yield
                  S.op("dve", lambda q: q.bn_aggr(smx[:, 8:10], smx[:, 0:6]), [f"sgs0{n}"], [f"sgs8{n}"])
                  yield
                  act(smx[:, 11:12], smx[:, 9:10], AF.Sqrt, [f"sgs8{n}", "epsc"], [f"sgs11{n}"], bias=epsc[:, 0:1], scale=1.0)
                  yield
                  recip(smx[:, 10:11], smx[:, 11:12], [f"sgs11{n}"], [f"sgs10{n}"])
                  yield
                  ts("dve", tBn[:, 0:256], rb[:, 256:512], smx[:, 8:9], smx[:, 10:11], ALU.subtract, ALU.mult,
                     [krb, f"sgs8{n}", f"sgs10{n}"], [ktB])
                  yield
                  tt("pool", tBn[:, 0:256], tBn[:, 0:256], rowb[:, 192:448], ALU.mult, [ktB, "rowb"], [ktB])
                  yield
                  tt("dve", vln, tBn[:, 0:256], rowb[:, 448:704], ALU.add, [ktB, "rowb", kx], [kx])
                  yield
                  pe_begin()
                  for gg in range(4):
                      mm(pZM[:, 256 + gg * 64:256 + (gg + 1) * 64], sgw[:, gg, :], vln[:, gg * 64:(gg + 1) * 64], True, True,
                         ["sgw", kx], [kZM])
                  pe_end()
                  yield
                  tt("dve", tBn[:, 256:512].rearrange("p (g d) -> p g d", g=4),
                     pZM[:, 256:512].rearrange("p (g d) -> p g d", g=4),
                     colp[:, 26:30].unsqueeze(2).to_broadcast([128, 4, 64]), ALU.add, [kZM, "colp"], [ktB])
                  yield
                  tt("pool", tBn[:, 256:512], tBn[:, 256:512], rb[:, 0:256], ALU.mult, [ktB, krb], [ktB])
                  yield
                  tt("dve", tBn[:, 256:512], tBn[:, 256:512], rc[:, 0:256], ALU.mult, [ktB, krc], [ktB])
                  yield
                  tt("pool", tCn[:, 0:256], tBn[:, 256:512], tBn[:, 256:512], ALU.mult, [ktB], [ktC])
                  yield
                  red(smx[:, 12:13], tCn[:, 0:256], ALU.add, [ktC], [f"sgs12{n}"])
                  yield
                  act(smx[:, 14:15], smx[:, 12:13], AF.Sqrt, [f"sgs12{n}", "epsc"], [f"sgs14{n}"], bias=epsc[:, 0:1],
                      scale=1.0 / 256)
                  yield
                  recip(smx[:, 13:14], smx[:, 14:15], [f"sgs14{n}"], [f"sgs13{n}"])
                  yield
                  stt(ysg, tBn[:, 256:512], smx[:, 13:14], rowb[:, 1216:1472], ALU.mult, ALU.mult,
                      [ktB, f"sgs13{n}", "rowb", kx], [kx])
                  yield
                  pe_begin()
                  for c in range(2):
                      tr(pTT[:, c * 128:(c + 1) * 128], ysg[:, c * 128:(c + 1) * 128], ident_b[:], [kx, "ident_b"], [kTT])
                  pe_end()
                  yield
                  cp("act", yT[:, 6:8, tc], pTT[:, 0:256].rearrange("p (c t) -> p c t", c=2), [kTT], [f"yT{g}"])
                  yield

              pipe((sgu_chain(ti) for ti in range(NT)), SGU_DEPTH)
              chk(5, l)

              if l == 0:
                  dump("yT_a", yT[:, :, :], [128, 8, T], BF16, [f"yT{g}" for g in range(NG)])

              chk(6, l)
              S.barrier(keep=GK + ("QT", "KT0", "Vstg"))
              cv = Carver()
              KTb = [KT0, cv.bf(SEQ)]
              Vpad = cv.bf(64 * 128).rearrange("p (t c) -> p t c", c=128)
              otb = [cv.f32(512) for _ in range(2)]
              o_all = cv.bf(64 * 128).rearrange("p (h t c) -> p h t c", h=2, c=64)
              endB = cv.off
              QT1 = cv.bf(SEQ)
              cv.off -= SEQ
              wout = cv.bf(8 * 1024).rearrange("p (k c) -> p k c", k=8)
              assert cv.off <= shared0
              QTb = [QT, QT1]
              load_q(1)
              zbufs = [tA, tB, tC, tA2, tB2, tC2]
              zgt = [zbufs[ti // 3][:, (ti % 3) * 256:(ti % 3 + 1) * 256].bitcast(BF16) for ti in range(NT)]
              spair = [(pAB, "pAB"), (pCD, "pCD")]
              obanks = [(pX, "pX"), (pK, "pK")]
              gpair = [0]

              S.op("pool", lambda q: q.memset(Vpad[:, :, 64:128], 0.0), [], ["Vpad"])
              for ti in range(NT):
                  dma("sp", zgt[ti], zgd.ap()[ti * 128:(ti + 1) * 128, :], "zgt", ["zgd"], [f"zgt{ti}"])
              for hh in range(2):
                  KT = KTb[hh]
                  kKT = f"KT{hh}"
                  QTx = QTb[hh]
                  kQT = "QT" if hh == 0 else "QT1"
                  Vh = Vpad
                  kVh = "Vpad"
                  cp("dve", Vpad[:, :, 0:65], Vstg[:, :, :], ["Vstg"], ["Vpad"])
                  if hh == 0:
                      load_k(1, "act")
                      load_v(1, "pool")
                  pairs = []
                  for qg in range(16):
                      nk = 4 * qg + 4
                      for kp in range(nk // 2):
                          pairs.append((qg, kp, nk))
                  pend = []

                  def emit_qk(pi, hh=hh, KT=KT, kKT=kKT, QTx=QTx, kQT=kQT):
                      qg, kp, nk = pairs[pi]
                      sb2, sbk = spair[gpair[0] % 2]
                      pt = ptb[gpair[0] % 3]
                      ptk = f"pt{gpair[0] % 3}"
                      gpair[0] += 1
                      pe_begin()
                      for h2 in range(2):
                          kt = 2 * kp + h2
                          mm(sb2[:, h2 * 512:(h2 + 1) * 512], KT[0:96, kt * 128:(kt + 1) * 128],
                             QTx[0:96, qg * 512:(qg + 1) * 512], True, True, [kKT, kQT], [sbk])
                      pe_end()
                      act(pt, sb2[:, :], AF.Exp, [sbk, "negM"], [ptk], bias=negM[:, 0:1], scale=SCALE)
                      if 2 * kp >= 4 * qg:
                          i0 = 2 * kp - 4 * qg
                          tt("dve", pt, pt, masks_b[:, i0:i0 + 2, :].rearrange("p a b -> p (a b)"), ALU.mult,
                             [ptk, "masks_b"], [ptk])
                      return pt, ptk

                  def emit_pv(pi, pt, ptk, hh=hh, Vh=Vh, kVh=kVh):
                      qg, kp, nk = pairs[pi]
                      ob, obk = obanks[qg % 2]
                      pe_begin()
                      for h2 in range(2):
                          kt = 2 * kp + h2
                          mm(ob[:, :], Vh[:, kt, :], pt[:, h2 * 512:(h2 + 1) * 512], kt == 0, kt == nk - 1,
                             [kVh, ptk], [obk])
                      pe_end()
                      if kp == nk // 2 - 1:
                          ot = otb[qg % 2]
                          otk = f"ot{qg % 2}"
                          cp("dve", ot[0:65, :], ob[0:65, :], [obk], [otk])

                          def fin(qg=qg, ot=ot, otk=otk, hh=hh):
                              pe_begin()
                              for i in range(4):
                                  tr(pS[:, i * 65:(i + 1) * 65], ot[0:65, i * 128:(i + 1) * 128], ident_f[0:65, 0:65],
                                     [otk, "ident_f"], ["pS"])
                              pe_end()
                              pv = pS[:, 0:260].rearrange("p (i c) -> p i c", c=65)
                              recip(sm[:, 48:52].unsqueeze(2), pv[:, :, 64:65], ["pS"], ["sm48"])
                              okey = "o_all0" if hh == 0 else ("o_all1a" if qg < 12 else "o_all1b")
                              tt("dve", o_all[:, hh, qg * 4:(qg + 1) * 4, :], pv[:, :, 0:64],
                                 sm[:, 48:52].unsqueeze(2).to_broadcast([128, 4, 64]), ALU.mult, ["pS", "sm48"],
                                 [okey])
                              if hh == 1 and qg == 11 and not (stop is not None and stop == (l, 7)):
                                  for t3 in range(3):
                                      dma("sp", sO1a.ap()[:, t3 * 1024:(t3 + 1) * 1024],
                                          o_all[:, 1, t3 * 16:(t3 + 1) * 16, :].rearrange("p t c -> p (t c)"),
                                          "o_all1a", ["o_all1a"], ["sO1a"])
                                  S.collective("pool", lambda q: q.collective_compute(
                                      "AllGather", ALU.bypass, replica_groups=RG, ins=[sO1a.ap().opt()],
                                      outs=[gO1a.ap().opt()]), ["sO1a"], ["gO1a"])
                          pend.append([2, fin])

                  prev = None
                  for pi in range(len(pairs)):
                      cur = emit_qk(pi)
                      if prev is not None:
                          emit_pv(pi - 1, *prev)
                      prev = cur
                      for it in list(pend):
                          it[0] -= 1
                          if it[0] <= 0:
                              it[1]()
                              pend.remove(it)
                  emit_pv(len(pairs) - 1, *prev)
                  for it in pend:
                      it[1]()
                  if stop is not None and stop == (l, 7):
                      continue
                  if hh == 0:
                      for t4 in range(4):
                          dma("sp", sO[0].ap()[:, t4 * 1024:(t4 + 1) * 1024],
                              o_all[:, 0, t4 * 16:(t4 + 1) * 16, :].rearrange("p t c -> p (t c)"), "o_all0",
                              ["o_all0"], ["sO0"])
                      S.collective("pool", lambda q: q.collective_compute(
                          "AllGather", ALU.bypass, replica_groups=RG, ins=[sO[0].ap().opt()], outs=[gO[0].ap().opt()]),
                          ["sO0"], ["gO0"])
                  else:
                      dma("sp", sO1b.ap()[:, :], o_all[:, 1, 48:64, :].rearrange("p t c -> p (t c)"), "o_all1b",
                          ["o_all1b"], ["sO1b"])
                      S.collective("pool", lambda q: q.collective_compute(
                          "AllGather", ALU.bypass, replica_groups=RG, ins=[sO1b.ap().opt()], outs=[gO1b.ap().opt()]),
                          ["sO1b"], ["gO1b"])
              dma("pool", wout[:], w_out[l].rearrange("(k p) c -> p k c", p=128), "wout", [], ["wout", "QT1"])
              chk(7, l)

              chk(8, l)
              S.barrier(keep=("wout",) + tuple(f"zgt{ti}" for ti in range(NT)))
              cv = Carver()
              xg2 = [cv.f32(8 * 512).rearrange("p (k t) -> p k t", k=8) for _ in range(2)]
              att = cv.bf(16 * 512).rearrange("p (a b t c) -> p a b t c", a=4, b=2, c=64)
              assert cv.off <= endB
              cvb = Carver()
              cvb.off = shared0
              tmpB = cvb.bf(4 * 1024).rearrange("p (a x) -> p a x", a=4)
              for hp in range(4):
                  S.dma("act", lambda q, hp=hp: q.dma_start(
                      out=att[:, hp, 0, :, :].rearrange("p t c -> p (t c)"),
                      in_=gO[0].ap()[hp * 128:(hp + 1) * 128, :][:, bass.ds(offs[("act", "o1024")], 1024)]),
                      "att", ["gO0"], ["att"])
              for hp in range(4):
                  S.dma("act", lambda q, hp=hp: q.dma_start(
                      out=att[:, hp, 1, :, :].rearrange("p t c -> p (t c)"),
                      in_=gO1a.ap()[hp * 128:(hp + 1) * 128, :][:, bass.ds(offs[("act", "oA")], 1024)]),
                      "att", ["gO1a"], ["att"])
              for hp in range(4):
                  dma("sp", tmpB[:, hp, :], gO1b.ap()[hp * 128:(hp + 1) * 128, :], "tmpB", ["gO1b"], ["tmpB"])
              att1v = att[:, :, 1, :, :].rearrange("p a t c -> p a (t c)")
              ts("dve", att1v, att1v, ofl[:, 0:1], None, ALU.mult, None, ["att", "ofl"], ["att"])
              stt(att1v, tmpB[:, :, :], ofl[:, 1:2], att1v, ALU.mult, ALU.add, ["tmpB", "ofl", "att"], ["att"])
              EP = [
                  dict(n="0", ra=r512a, rb=r512b, kra="r512a", krb="r512b", xf=xfin, kx="xfin", sm=sm[:, 52:56],
                       pTT=pT, kTT="pT"),
                  dict(n="1", ra=r512d, rb=r512e, kra="r512d", krb="r512e", xf=xfin2, kx="xfin2", sm=sm2[:, 16:20],
                       pTT=pK[:, :].bitcast(BF16), kTT="pK"),
              ]

              def epi_chain(ti):
                  P = EP[(ti % 2) * EPI_SETS]
                  n = P["n"]
                  g = ti // 4
                  tc = slice(ti * 128, (ti + 1) * 128)
                  ra, rb, kra, krb, kx, smx, pTT, kTT = (P[k] for k in ("ra", "rb", "kra", "krb", "kx", "sm", "pTT", "kTT"))
                  yo = P["xf"][:].rearrange("p h d -> p (h d)")[:, 0:512]
                  tt("dve", ra[:].rearrange("p (a b c) -> p a b c", a=4, b=2), att[:, :, :, ti, :],
                     zgt[ti].rearrange("p (a b c) -> p a b c", a=4, b=2), ALU.mult, ["att", f"zgt{ti}"], [kra])
                  yield
                  tt("pool", rb[:], ra[:], ra[:], ALU.mult, [kra], [krb])
                  yield
                  red(smx[:, 0:1], rb[:], ALU.add, [krb], [f"sme0{n}"])
                  yield
                  act(smx[:, 2:3], smx[:, 0:1], AF.Sqrt, [f"sme0{n}", "epsc"], [f"sme2{n}"], bias=epsc[:, 0:1],
                      scale=1.0 / 512)
                  yield
                  recip(smx[:, 1:2], smx[:, 2:3], [f"sme2{n}"], [f"sme1{n}"])
                  yield
                  stt(yo, ra[:], smx[:, 1:2], rowb[:, 704:1216], ALU.mult, ALU.mult, [kra, f"sme1{n}", "rowb", kx], [kx])
                  yield
                  pe_begin()
                  for c in range(4):
                      tr(pTT[:, c * 128:(c + 1) * 128], yo[:, c * 128:(c + 1) * 128], ident_b[:], [kx, "ident_b"], [kTT])
                  pe_end()
                  yield
                  cp("act", yT[:, 2:6, tc], pTT[:, 0:512].rearrange("p (c t) -> p c t", c=4), [kTT], [f"yT{g}"])
                  yield

              obl = [(pA, "pA"), (pB, "pB"), (pC, "pC"), (pD, "pD")]

              def outproj_chain(g):
                  gc = slice(g * 512, (g + 1) * 512)
                  xb = xg2[g % 2]
                  kxb = f"xg2_{g % 2}"
                  dma("sp", xb[:], xs3[:, :, gc], kxb, ["xres"] if l > 0 else [], [kxb])
                  yield
                  for co in range(8):
                      pb, pbk = obl[co % 4]
                      for kc in range(8):
                          mm(pb[:, :], wout[:, kc, co * 128:(co + 1) * 128], yT[:, kc, gc], kc == 0, kc == 7,
                             ["wout", f"yT{g}"], [pbk])
                      tt("dve", xb[:, co, :], pb[:, :], xb[:, co, :], ALU.add, [pbk, kxb], [kxb])
                      yield
                  dma("sp", xd3[:, :, gc], xb[:], kxb, [kxb], ["xres" if l < depth - 1 else "outT"])
                  if l < depth - 1 and g == NG - 1:
                      dma("sp", sTl.ap().rearrange("(k p) t -> p k t", p=128), xb[:, :, 480:512], kxb, [kxb], ["sTl"])
                  yield

              def epi_all():
                  for g in range(NG):
                      for i in range(4):
                          yield epi_chain(g * 4 + i)
                          if i == 0 and g > 0:
                              yield outproj_chain(g - 1)
                  yield None
                  yield outproj_chain(NG - 1)

              pipe(epi_all(), EPI_DEPTH)
              if l == 0:
                  dump("yT_b", yT[:, :, :], [128, 8, T], BF16, [f"yT{g}" for g in range(NG)])
                  dump("xres", xres.ap(), [D, T], F32, ["xres"])
              if l < depth - 1:
                  S.collective("pool", lambda q: q.collective_compute(
                      "AllGather", ALU.bypass, replica_groups=RG, ins=[sTl.ap().opt()], outs=[gTl.ap().opt()]),
                      ["sTl"], ["gTl"])

        except _Stop:
            pass
        keys = ["outT"] + ["dbg_" + k for k in dbg]
        assert not pe_pend and pe_depth[0] == 0
        S.wait_all("sp", keys)
        S.barrier()
        S.emit()
    return nc, list(dbg.keys())


def _host_consts():
    ident = np.eye(128, dtype=np.float32)
    k = np.arange(128)[:, None]
    q = np.arange(512)[None, :]
    masks = np.concatenate([(q >= 128 * i + k).astype(np.float32) for i in range(4)], axis=1)
    return np.ascontiguousarray(np.concatenate([ident, masks], axis=1))


def _rope_table(pos0):
    half = 16
    inv_freq = (np.float32(10000.0) ** (-np.arange(half, dtype=np.float32) / np.float32(half))).astype(np.float32)
    pos = (pos0 + np.arange(T, dtype=np.float32)).astype(np.float32)
    ang = (pos[:, None] * inv_freq[None, :]).astype(np.float32)
    cs = np.concatenate([np.cos(ang), np.sin(ang)], axis=1).astype(np.float32)
    return np.ascontiguousarray(cs.reshape(NT, 128, 32).transpose(1, 0, 2).reshape(128, NT * 32))


def _pack_params(p):
    L = 2
    colp = np.zeros((L, 128, NCOL), np.float32)
    rowp = np.zeros((L, NROW), np.float32)

    def cols(v, n):
        return np.asarray(v, np.float32).reshape(n, 128).T

    for l in range(L):
        colp[l, :, 0:8] = cols(p["norm_g"][l], 8)
        colp[l, :, 8:10] = cols(p["conv_b"][l], 2)
        colp[l, :, 10:12] = cols(p["conv_ln_g"][l], 2)
        colp[l, :, 12:14] = cols(p["conv_ln_b"][l], 2)
        colp[l, :, 14:16] = cols(p["conv_pw_b"][l], 2)
        colp[l, :, 16:22] = cols(p["q_norm_g"][l], 6)
        colp[l, :, 22:24] = cols(p["kv_norm_g"][l], 2)
        colp[l, :, 24:26] = cols(p["branch_norm_g"][l][0:256], 2)
        colp[l, :, 26:30] = np.asarray(p["sg_b"][l], np.float32).T
        cw = np.asarray(p["conv_w"][l], np.float32)
        for ct in range(2):
            colp[l, :, 30 + ct * 31:30 + (ct + 1) * 31] = cw[:, ct * 128:(ct + 1) * 128].T
        rowp[l, 0:96] = p["qk_q_g"][l]
        rowp[l, 96:192] = p["qk_k_g"][l]
        rowp[l, 192:448] = p["sg_ln_g"][l]
        rowp[l, 448:704] = p["sg_ln_b"][l]
        rowp[l, 704:1216] = p["branch_norm_g"][l][256:768]
        rowp[l, 1216:1472] = p["branch_norm_g"][l][768:1024]
    return colp, rowp


_CACHE = {}


def make_in_maps(inputs):
    p = {k: np.asarray(v) for k, v in inputs.items()}
    x = p["x"].astype(np.float32)
    colp, rowp = _pack_params(p)
    cst = _host_consts()
    sgwT = np.ascontiguousarray(np.transpose(p["sg_w"].astype(np.float32), (0, 3, 1, 2)).reshape(2, 128, 512))
    shared = {
        "w_in": np.ascontiguousarray(p["w_in"], dtype=np.float32),
        "w_uq": np.ascontiguousarray(p["w_uq"], dtype=np.float32),
        "w_ukv": np.ascontiguousarray(p["w_ukv"], dtype=np.float32),
        "w_out": np.ascontiguousarray(p["w_out"], dtype=np.float32),
        "pw": np.ascontiguousarray(p["conv_pw_w"], dtype=np.float32),
        "sgwT": sgwT, "colp": colp, "rowp": rowp, "cst": cst,
    }
    maps = []
    for c in range(NCORES):
        b, j = c // 4, c % 4
        xs = x[b, j * T:(j + 1) * T, :]
        m = dict(shared)
        m["xT"] = np.ascontiguousarray(xs.T)
        if j == 0:
            m["xh"] = np.zeros((D, 32), np.float32)
        else:
            m["xh"] = np.ascontiguousarray(x[b, j * T - 32:j * T, :].T)
        m["posoff"] = np.full((128, 1), float(j * T), np.float32)
        m["idx"] = np.array([[j * 192, j * 256, j * T, max(j - 1, 0) * D, j * 1024, min(j, 2) * 1024, 0, 0]], dtype=np.int32)
        m["ofl"] = np.tile(np.array([[1.0, 0.0]] if j < 3 else [[0.0, 1.0]], np.float32), (128, 1))
        m["cflag"] = np.full((128, 1), 1.0 if j > 0 else 0.0, np.float32)
        maps.append(m)
    return maps


def kernel(**inputs):
    if "nc" not in _CACHE:
        _CACHE["nc"] = build_program(depth=2, debug=False)[0]
    nc = _CACHE["nc"]
    maps = make_in_maps(inputs)
    res = run_bass_kernel_spmd(nc, maps, core_ids=list(range(NCORES)))
    out = np.empty((2, SEQ, D), np.float32)
    for c in range(NCORES):
        b, j = c // 4, c % 4
        out[b, j * T:(j + 1) * T, :] = np.asarray(res.results[c]["outT"]).T
    return out
```

```python
import os
import numpy as np
from contextlib import ExitStack
import concourse.bass as bass
import concourse.mybir as mybir
from concourse.bass_utils import run_bass_kernel_spmd

F32 = mybir.dt.float32
BF16 = mybir.dt.bfloat16
I32 = mybir.dt.int32
ALU = mybir.AluOpType
AF = mybir.ActivationFunctionType
AX = mybir.AxisListType

NCORES = 8
T = 2048
NG = 4
NT = 16
SEQ = 8192
D = 1024
INC = 3104
EPS = 1e-6
SCALE = 96 ** -0.5
NCOL = 92
NROW = 1472
SEM_EPOCH = 30000
SGU_DEPTH = 2
EPI_DEPTH = 2
P1_DEPTH = 2
EPI_SETS = 1
GELU_C = 1.5957691216057308

C_CONV = (0, 768)
C_Q = (768, 1536)
C_KV = (1536, 1824)
C_ZM = (1824, 2336)
C_SG = (2336, 3104)


class _Eng:
    def __init__(self, name):
        self.name = name
        self.ops = []
        self.seen = {}
        self.sem = None
        self.cnt = 0


class Sched:
    ENGINES = ("pe", "act", "dve", "pool", "sp")

    def __init__(self, nc, stack):
        self.nc = nc
        self.stack = stack
        self.E = {n: _Eng(n) for n in self.ENGINES}
        self.last_write = {}
        self.readers = {}
        self.dma_sems = {}
        self.all_sems = []
        self.nsem = 0
        for n in ("pe", "act", "dve", "pool"):
            self._new_epoch(self.E[n])

    def _alloc_sem(self, name):
        self.nsem += 1
        s = self.stack.enter_context(self.nc.semaphore(f"{name}_{self.nsem}"))
        return s

    def _new_epoch(self, e):
        e.sem = self._alloc_sem("c_" + e.name)
        e.cnt = 0

    def _deps(self, reads, writes):
        deps = []
        for k in reads:
            t = self.last_write.get(k)
            if t is not None:
                deps.append(t)
        for k in writes:
            t = self.last_write.get(k)
            if t is not None:
                deps.append(t)
            deps.extend(self.readers.get(k, ()))
        return deps

    def _commit(self, tok, reads, writes):
        for k in writes:
            self.last_write[k] = tok
            self.readers[k] = []
        for k in reads:
            if k not in writes:
                self.readers.setdefault(k, []).append(tok)

    def _waits(self, e, deps, skip_sem=None):
        need = {}
        for (s, v) in deps:
            if skip_sem is not None and s is skip_sem:
                continue
            if v > need.get(id(s), (None, 0))[1]:
                need[id(s)] = (s, v)
        out = []
        for sid, (s, v) in need.items():
            if v > e.seen.get(sid, 0):
                e.seen[sid] = v
                out.append((s, v))
        return out

    def op(self, eng, fn, reads=(), writes=()):
        e = self.E[eng]
        reads = list(reads)
        writes = list(writes)
        deps = self._deps(reads, writes)
        waits = self._waits(e, deps, skip_sem=e.sem if eng == "pe" else None)
        if e.cnt >= SEM_EPOCH:
            self._new_epoch(e)
        e.cnt += 1
        sem, val = e.sem, e.cnt

        def run(q, fn=fn, waits=waits, sem=sem):
            for (s, v) in waits:
                q.wait_ge(s, v)
            fn(q).then_inc(sem, 1)
        e.ops.append(run)
        tok = (sem, val)
        self._commit(tok, reads, writes)
        return tok

    def dma(self, eng, fn, semkey, reads=(), writes=()):
        e = self.E[eng]
        reads = list(reads)
        writes = list(writes)
        deps = self._deps(reads, writes)
        waits = self._waits(e, deps)
        ent = self.dma_sems.get(semkey)
        if ent is None:
            ent = [self._alloc_sem("d"), 0]
            self.dma_sems[semkey] = ent
        ent[1] += 16
        sem, val = ent[0], ent[1]

        def run(q, fn=fn, waits=waits, sem=sem):
            for (s, v) in waits:
                q.wait_ge(s, v)
            fn(q).then_inc(sem, 16)
        e.ops.append(run)
        tok = (sem, val)
        self._commit(tok, reads, writes)
        return tok

    def collective(self, eng, fn, reads=(), writes=()):
        e = self.E[eng]
        reads = list(reads)
        writes = list(writes)
        deps = self._deps(reads, writes)
        waits = self._waits(e, deps)
        sem = self._alloc_sem("cc")

        def run(q, fn=fn, waits=waits, sem=sem):
            for (s, v) in waits:
                q.wait_ge(s, v)
            fn(q).then_inc(sem)
        e.ops.append(run)
        tok = (sem, 1)
        self._commit(tok, reads, writes)
        return tok

    def wait_all(self, eng, keys):
        e = self.E[eng]
        deps = []
        for k in keys:
            if k in self.last_write:
                deps.append(self.last_write[k])
            deps.extend(self.readers.get(k, ()))
        waits = self._waits(e, deps)

        def run(q, waits=waits):
            for (s, v) in waits:
                q.wait_ge(s, v)
        e.ops.append(run)

    def barrier(self, keep=()):
        kept = {k: self.last_write[k] for k in keep if k in self.last_write}
        toks = []
        for n in ("pe", "act", "dve", "pool"):
            e = self.E[n]
            if e.cnt > 0:
                toks.append((e.sem, e.cnt))
        for k, (s, v) in self.dma_sems.items():
            toks.append((s, v))
        for k, t in self.last_write.items():
            if k not in kept:
                toks.append(t)
        for n in self.ENGINES:
            e = self.E[n]
            waits = self._waits(e, toks)

            def run(q, waits=waits):
                for (s, v) in waits:
                    q.wait_ge(s, v)
            e.ops.append(run)
        self.last_write = dict(kept)
        self.readers = {}

    def custom(self, eng, run):
        self.E[eng].ops.append(run)

    def emit(self):
        nc = self.nc
        with nc.Block() as block:
            @block.tensor
            def _(q):
                for f in self.E["pe"].ops:
                    f(q)

            @block.scalar
            def _(q):
                for f in self.E["act"].ops:
                    f(q)

            @block.vector
            def _(q):
                for f in self.E["dve"].ops:
                    f(q)

            @block.gpsimd
            def _(q):
                for f in self.E["pool"].ops:
                    f(q)

            @block.sync
            def _(q):
                for f in self.E["sp"].ops:
                    f(q)


class _Stop(Exception):
    pass


def build_program(depth=2, debug=False, stop=None):
    nc = bass.Bass("TRN2", target_bir_lowering=False)
    dt = nc.dram_tensor
    xT_in = dt("xT", [D, T], F32, kind="ExternalInput").ap()
    xh_in = dt("xh", [D, 32], F32, kind="ExternalInput").ap()
    w_in = dt("w_in", [2, D, INC], F32, kind="ExternalInput").ap()
    w_uq = dt("w_uq", [2, 768, 768], F32, kind="ExternalInput").ap()
    w_ukv = dt("w_ukv", [2, 256, 1024], F32, kind="ExternalInput").ap()
    w_out = dt("w_out", [2, D, D], F32, kind="ExternalInput").ap()
    pw_in = dt("pw", [2, 256, 256], F32, kind="ExternalInput").ap()
    sgw_in = dt("sgwT", [2, 128, 512], F32, kind="ExternalInput").ap()
    colp_in = dt("colp", [2, 128, NCOL], F32, kind="ExternalInput").ap()
    rowp_in = dt("rowp", [2, NROW], F32, kind="ExternalInput").ap()
    cst_in = dt("cst", [128, 128 + 2048], F32, kind="ExternalInput").ap()
    posoff_in = dt("posoff", [128, 1], F32, kind="ExternalInput").ap()
    idx_in = dt("idx", [1, 8], I32, kind="ExternalInput").ap()
    cflag_in = dt("cflag", [128, 1], F32, kind="ExternalInput").ap()
    ofl_in = dt("ofl", [128, 2], F32, kind="ExternalInput").ap()
    outT = dt("outT", [D, T], F32, kind="ExternalOutput").ap()

    sQ = [dt(f"sQ{c}", [768, 512], BF16) for c in range(4)]
    sK = [dt(f"sK{c}", [768, 512], BF16) for c in range(4)]
    sV = [dt(f"sV{c}", [1024, 4 * 65], BF16) for c in range(4)]
    gQ = [dt(f"gQ{c}", [4 * 768, 512], BF16) for c in range(4)]
    gK = [dt(f"gK{c}", [4 * 768, 512], BF16) for c in range(4)]
    gV = [dt(f"gV{c}", [4 * 1024, 4 * 65], BF16) for c in range(4)]
    sO = [dt(f"sO{h}", [128, 4096], BF16) for h in range(2)]
    gO = [dt(f"gO{h}", [4 * 128, 4096], BF16) for h in range(2)]
    sO1a = dt("sO1a", [128, 3072], BF16)
    gO1a = dt("gO1a", [4 * 128, 3072], BF16)
    sO1b = dt("sO1b", [128, 1024], BF16)
    gO1b = dt("gO1b", [4 * 128, 1024], BF16)
    zgd = dt("zgd", [T, 512], BF16)
    xres = dt("xres", [D, T], F32)
    sTl = dt("sTl", [D, 32], F32)
    gTl = dt("gTl", [4 * D, 32], F32)
    dbg = {}

    with ExitStack() as st:
        S = Sched(nc, st)
        sb = lambda name, shape, dtype: st.enter_context(nc.sbuf_tensor(name, shape, dtype))
        ps = lambda name, shape, dtype: st.enter_context(nc.psum_tensor(name, shape, dtype))

        ident_b = sb("ident_b", [128, 128], BF16)
        ident_f = sb("ident_f", [128, 128], F32)
        masks_b = sb("masks_b", [128, 4, 512], BF16)
        cs_t = sb("cs_t", [128, NT, 32], F32)
        colp = sb("colp_t", [128, NCOL], F32)
        rowb = sb("rowb", [128, NROW], F32)
        cflag = sb("cflag_t", [128, 1], F32)
        epsc = sb("epsc", [128, 1], F32)
        negM = sb("negM", [128, 1], F32)
        onesm = sb("onesm", [128, 128], BF16)
        ones256 = sb("ones256", [128, 128], BF16)
        onescol = sb("onescol", [128, 2], BF16)
        yT = sb("yT", [128, 8, T], BF16)
        pww = sb("pww", [128, 2, 256], BF16)
        sgw = sb("sgw", [128, 4, 128], BF16)
        rstdq = sb("rstdq", [128, NT], F32)
        rstdkv = sb("rstdkv", [128, NT], F32)
        sm = sb("sm", [128, 64], F32)
        tA = sb("tA", [128, 768], F32)
        tB = sb("tB", [128, 768], F32)
        tC = sb("tC", [128, 768], F32)
        tD = sb("tD", [128, 2, 8, 16], F32)
        tE = sb("tE", [128, 2, 8, 16], F32)
        tD2 = sb("tD2", [128, 2, 8, 16], F32)
        tE2 = sb("tE2", [128, 2, 8, 16], F32)
        xfin = sb("xfin", [128, 8, 96], BF16)
        tA2 = sb("tA2", [128, 768], F32)
        tB2 = sb("tB2", [128, 768], F32)
        tC2 = sb("tC2", [128, 768], F32)
        xfin2 = sb("xfin2", [128, 8, 96], BF16)
        sm2 = sb("sm2", [128, 32], F32)
        r512a = sb("r512a", [128, 512], F32)
        r512b = sb("r512b", [128, 512], F32)
        r512c = sb("r512c", [128, 512], F32)
        r512d = sb("r512d", [128, 512], F32)
        r512e = sb("r512e", [128, 512], F32)
        r512f = sb("r512f", [128, 512], F32)
        ARENA = 116 * 512
        arena = sb("arena", [128, ARENA], BF16)

        pAB = ps("pAB", [128, 1024], F32)
        pCD = ps("pCD", [128, 1024], F32)
        pA = pAB[:, 0:512]
        pB = pAB[:, 512:1024]
        pC = pCD[:, 0:512]
        pD = pCD[:, 512:1024]
        pS = ps("pS", [128, 512], F32)
        pK = ps("pK", [128, 512], F32)
        pX = ps("pX", [128, 512], F32)
        pT = ps("pT", [128, 1024], BF16)

        class Carver:
            def __init__(self):
                self.off = 0

            def take(self, nelem_bf16):
                a = self.off
                self.off += nelem_bf16
                assert self.off <= ARENA, (self.off, ARENA)
                return arena[:, a:a + nelem_bf16]

            def bf(self, n):
                return self.take(n)

            def f32(self, n):
                return self.take(2 * n).bitcast(F32)

        pe_pend = []
        pe_depth = [0]

        def pe_flush():
            if not pe_pend:
                return
            fns = [f for (f, _, _) in pe_pend]
            rds, wrs = [], []
            for (_, r, w) in pe_pend:
                for k in r:
                    if k not in rds:
                        rds.append(k)
                for k in w:
                    if k not in wrs:
                        wrs.append(k)
            pe_pend.clear()

            def run(q, fns=fns):
                for f in fns[:-1]:
                    f(q)
                return fns[-1](q)
            S.op("pe", run, rds, wrs)

        def pe_begin():
            pe_depth[0] += 1

        def pe_end():
            pe_depth[0] -= 1
            if pe_depth[0] == 0:
                pe_flush()

        def mm(out, lhsT, rhs, start, stop, reads, writes):
            pe_pend.append((lambda q: q.matmul(out, lhsT, rhs, start=start, stop=stop), list(reads), list(writes)))
            if stop and pe_depth[0] == 0:
                pe_flush()

        def tr(out, in_, ident, reads, writes):
            pe_pend.append((lambda q: q.transpose(out, in_, ident), list(reads), list(writes)))
            if pe_depth[0] == 0:
                pe_flush()

        def act(out, in_, func, reads, writes, bias=None, scale=None):
            kw = {}
            if bias is not None:
                kw["bias"] = bias
            if scale is not None:
                kw["scale"] = scale
            S.op("act", lambda q: q.activation(out, in_, func, **kw), reads, writes)

        def tt(eng, out, in0, in1, op, reads, writes):
            S.op(eng, lambda q: q.tensor_tensor(out, in0, in1, op), reads, writes)

        def ts(eng, out, in0, s1, s2, op0, op1, reads, writes):
            if op1 is None:
                S.op(eng, lambda q: q.tensor_scalar(out, in0, s1, None, op0), reads, writes)
            else:
                S.op(eng, lambda q: q.tensor_scalar(out, in0, s1, s2, op0, op1), reads, writes)

        def stt(out, in0, scalar, in1, op0, op1, reads, writes):
            S.op("dve", lambda q: q.scalar_tensor_tensor(out, in0, scalar, in1, op0, op1), reads, writes)

        def cp(eng, out, in_, reads, writes):
            if eng == "act":
                S.op(eng, lambda q: q.activation(out, in_, AF.Identity), reads, writes)
            else:
                S.op(eng, lambda q: q.tensor_copy(out, in_), reads, writes)

        def recip(out, in_, reads, writes):
            S.op("dve", lambda q: q.reciprocal(out, in_), reads, writes)

        def red(out, in_, op, reads, writes, absv=False):
            if absv:
                S.op("dve", lambda q: q.tensor_reduce(out, in_, AX.X, op, apply_absolute_value=True), reads, writes)
            else:
                S.op("dve", lambda q: q.tensor_reduce(out, in_, AX.X, op), reads, writes)

        def rsqrt(out, in_, scale, reads, writes, tmpkey="sm_rs", tmp=None):
            if tmp is None:
                act(out, in_, AF.Sqrt, reads, writes, bias=epsc[:, 0:1], scale=scale)
                recip(out, out, writes, writes)
            else:
                act(tmp, in_, AF.Sqrt, reads, [tmpkey], bias=epsc[:, 0:1], scale=scale)
                recip(out, tmp, [tmpkey], writes)

        def dma(eng, out, in_, semkey, reads, writes):
            S.dma(eng, lambda q: q.dma_start(out=out, in_=in_), semkey, reads, writes)

        def pipe(gens, depth):
            gens = iter(gens)
            active = []
            done = False
            while True:
                while not done and len(active) < depth:
                    try:
                        nx = next(gens)
                    except StopIteration:
                        done = True
                        break
                    if nx is None:
                        for gn in active:
                            for _ in gn:
                                pass
                        active = []
                        continue
                    active.append(nx)
                if not active:
                    break
                for gn in list(active):
                    try:
                        next(gn)
                    except StopIteration:
                        active.remove(gn)


        regs = {}
        offs = {}

        IDX_NAMES = ("o192", "o256", "o2048", "oT", "o1024", "oA")
        QREGS = {"sp": ("o192", "o256", "oT"), "act": ("o192", "o1024", "oA"), "pool": ("o256",)}

        def mk_load_regs(qn):
            def load_regs(q):
                for nm in QREGS[qn]:
                    i = IDX_NAMES.index(nm)
                    r = st.enter_context(q.register("r_" + qn + nm))
                    q.reg_load(r, idx_in[0:1, i:i + 1])
                    offs[(qn, nm)] = q.snap(r)
            return load_regs
        for qn in ("sp", "act", "pool"):
            S.custom(qn, mk_load_regs(qn))

        dma("pool", ident_b[:], cst_in[:, 0:128], "c0", [], ["ident_b"])
        dma("pool", masks_b[:].rearrange("p a b -> p (a b)"), cst_in[:, 128:2176], "c1", [], ["masks_b"])
        dma("sp", ident_f[:], cst_in[:, 0:128], "c2", [], ["ident_f"])
        import math
        posoff = sb("posoff_t", [128, 1], F32)
        dma("sp", posoff[:], posoff_in[:, :], "c3", [], ["posoff"])
        pos_i = r512a[:, 256:256 + NT].bitcast(I32)
        pos_f = r512a[:, 288:288 + NT]
        invf = r512a[:, 320:336]
        ang = r512a[:, 0:256].rearrange("p (t i) -> p t i", i=16)
        kq_i = r512b[:, 0:256].bitcast(I32).rearrange("p (t i) -> p t i", i=16)
        kq_f = r512b[:, 256:512].rearrange("p (t i) -> p t i", i=16)
        red_t = r512c[:, 0:256].rearrange("p (t i) -> p t i", i=16)
        arg_t = r512c[:, 256:512].rearrange("p (t i) -> p t i", i=16)
        S.op("pool", lambda q: q.iota(pos_i, [[128, NT]], base=0, channel_multiplier=1), [], ["pos_i"])
        cp("dve", pos_f[:], pos_i[:], ["pos_i"], ["pos_f"])
        ts("dve", pos_f[:], pos_f[:], posoff[:, 0:1], None, ALU.add, None, ["pos_f", "posoff"], ["pos_f"])
        inv_freq = (np.float32(10000.0) ** (-np.arange(16, dtype=np.float32) / np.float32(16))).astype(np.float32)
        for i in range(16):
            S.op("dve", lambda q, i=i: q.memset(invf[:, i:i + 1], float(inv_freq[i])), [], ["invf"])
        tt("dve", ang[:], pos_f[:].unsqueeze(2).to_broadcast([128, NT, 16]),
           invf[:].unsqueeze(1).to_broadcast([128, NT, 16]), ALU.mult, ["pos_f", "invf"], ["ang"])
        ts("dve", kq_f[:], ang[:], 1.0 / (2.0 * math.pi), None, ALU.mult, None, ["ang"], ["kq_f"])
        cp("dve", kq_i[:], kq_f[:], ["kq_f"], ["kq_i"])
        cp("dve", kq_f[:], kq_i[:], ["kq_i"], ["kq_f"])
        C1 = 6.28125
        C2 = float(np.float32(2.0 * math.pi - C1))
        C3 = float(2.0 * math.pi - C1 - C2)
        stt(red_t[:], kq_f[:], -C1, ang[:], ALU.mult, ALU.add, ["ang", "kq_f"], ["red_t"])
        stt(red_t[:], kq_f[:], -C2, red_t[:], ALU.mult, ALU.add, ["red_t", "kq_f"], ["red_t"])
        stt(red_t[:], kq_f[:], -C3, red_t[:], ALU.mult, ALU.add, ["red_t", "kq_f"], ["red_t"])
        PI_LO = 3.1415925
        TWO_PI = 2.0 * math.pi
        wrp = r512d[:, 0:256].rearrange("p (t i) -> p t i", i=16)
        for shift, lo in ((math.pi / 2.0, 0), (0.0, 16)):
            ts("dve", arg_t[:], red_t[:], shift, None, ALU.add, None, ["red_t"], ["arg_t"])
            for _ in range(2):
                ts("dve", wrp[:], arg_t[:], math.pi, TWO_PI, ALU.is_gt, ALU.mult, ["arg_t"], ["wrp"])
                tt("dve", arg_t[:], arg_t[:], wrp[:], ALU.subtract, ["arg_t", "wrp"], ["arg_t"])
                ts("dve", wrp[:], arg_t[:], -math.pi, TWO_PI, ALU.is_lt, ALU.mult, ["arg_t"], ["wrp"])
                tt("dve", arg_t[:], arg_t[:], wrp[:], ALU.add, ["arg_t", "wrp"], ["arg_t"])
            ts("dve", arg_t[:], arg_t[:], PI_LO, -PI_LO, ALU.min, ALU.max, ["arg_t"], ["arg_t"])
            act(cs_t[:, :, lo:lo + 16], arg_t[:], AF.Sin, ["arg_t"], ["cs_t"])
        dma("sp", cflag[:], cflag_in[:, :], "c4", [], ["cflag"])
        ofl = sb("ofl_t", [128, 2], F32)
        dma("sp", ofl[:], ofl_in[:, :], "c5", [], ["ofl"])
        S.op("dve", lambda q: q.memset(epsc[:], EPS), [], ["epsc"])
        S.op("dve", lambda q: q.memset(onesm[:], 1.0 / 1024), [], ["onesm"])
        S.op("dve", lambda q: q.memset(ones256[:], 1.0 / 256), [], ["ones256"])
        S.op("dve", lambda q: q.memset(onescol[:, 0:1], 1.0 / 256), [], ["onescol"])
        S.op("dve", lambda q: q.memset(onescol[:, 1:2], 1.0 / 768), [], ["onescol"])

        def dump(name, ap_sb, shape, dtype, reads):
            if not debug:
                return
            d = dt("dbg_" + name, shape, dtype, kind="ExternalOutput").ap()
            dbg[name] = d
            dma("sp", d, ap_sb, "dbg_" + name, reads, ["dbg_" + name])

        dump("cs", cs_t[:], [128, NT, 32], F32, ["cs_t"])

        def chk(level, l):
            if stop is not None and stop == (l, level):
                raise _Stop()

        try:
          for l in range(depth):
              x_src = xT_in if l == 0 else xres.ap()
              x_dst = outT if l == depth - 1 else xres.ap()
              xs3 = x_src.rearrange("(k p) t -> p k t", p=128)
              xd3 = x_dst.rearrange("(k p) t -> p k t", p=128)
              L = f"L{l}"

              if l > 0:
                  S.barrier(keep=("gTl",))
              cv = Carver()
              hT = cv.bf(8 * 2080).rearrange("p (k t) -> p k t", k=8)
              wbufA = cv.bf(8 * 768).rearrange("p (k c) -> p k c", k=8)
              wbufB = cv.bf(8 * 768).rearrange("p (k c) -> p k c", k=8)
              wuq = cv.bf(6 * 768).rearrange("p (k c) -> p k c", k=6)
              wukv = cv.bf(2 * 1024).rearrange("p (k c) -> p k c", k=2)
              shared0 = cv.off
              xgb = [cv.f32(8 * 512).rearrange("p (k t) -> p k t", k=8) for _ in range(2)]
              sq8 = cv.bf(8 * 512).rearrange("p (k t) -> p k t", k=8)
              assert cv.off <= ARENA
              cv.off = shared0
              stgQ = cv.bf(8 * 512).rearrange("p (h t) -> p h t", h=8)
              stgK = cv.bf(8 * 512).rearrange("p (h t) -> p h t", h=8)
              vstb = [cv.bf(8 * 4 * 65).rearrange("p (h t c) -> p h t c", h=8, t=4) for _ in range(2)]
              cqTb = [cv.bf(6 * 512).rearrange("p (k t) -> p k t", k=6) for _ in range(2)]
              ckvTb = [cv.bf(2 * 512).rearrange("p (k t) -> p k t", k=2) for _ in range(2)]
              csq = cv.bf(6 * 512).rearrange("p (k t) -> p k t", k=6)
              endA = cv.off

              dma("sp", colp[:], colp_in[l], "colp", [], ["colp"])
              dma("sp", rowb[:], rowp_in[l:l + 1, :].to_broadcast([128, NROW]), "rowb", [], ["rowb"])
              win3 = w_in[l].rearrange("(k p) c -> p k c", p=128)
              dma("pool", wbufA[:, :, 0:768], win3[:, :, C_Q[0]:C_Q[1]], "wbufA", [], ["wbufA"])
              dma("pool", wbufB[:, :, 0:288], win3[:, :, C_KV[0]:C_KV[1]], "wbufB", [], ["wbufB"])
              dma("pool", wuq[:], w_uq[l].rearrange("(k p) c -> p k c", p=128), "wuq", [], ["wuq"])
              dma("pool", wukv[:], w_ukv[l].rearrange("(k p) c -> p k c", p=128), "wukv", [], ["wukv"])
              dma("pool", pww[:], pw_in[l].rearrange("(k p) c -> p k c", p=128), "pww", [], ["pww"])
              dma("pool", sgw[:].rearrange("p g t -> p (g t)"), sgw_in[l], "sgw", [], ["sgw"])
              tt("dve", sgw[:], sgw[:], masks_b[:, 0, 0:128].unsqueeze(1).to_broadcast([128, 4, 128]),
                 ALU.mult, ["sgw", "masks_b"], ["sgw"])
              red(sm[:, 60:61], rowb[:, 0:96], ALU.max, ["rowb"], ["sm60"], absv=True)
              red(sm[:, 61:62], rowb[:, 96:192], ALU.max, ["rowb"], ["sm61"], absv=True)
              tt("dve", sm[:, 62:63], sm[:, 60:61], sm[:, 61:62], ALU.mult, ["sm60", "sm61"], ["sm62"])
              ts("dve", negM[:], sm[:, 62:63], -96.0 * SCALE, None, ALU.mult, None, ["sm62"], ["negM"])

              P1S = [dict(xg=xgb[0], kxg="xg0", rs=r512a, krs="r512a", ps=pS, kps="pS"),
                     dict(xg=xgb[1], kxg="xg1", rs=r512d, krs="r512d", ps=pK, kps="pK")]

              def p1_chain(idx, load_fn, n, hcols, key, halo=False):
                  P = P1S[idx % 2]
                  xgx, kxg, rs, krs, psx, kps = (P[k] for k in ("xg", "kxg", "rs", "krs", "ps", "kps"))
                  S.dma("sp", lambda q: load_fn(q, xgx[:, :, 0:n]), kxg, ["gTl"] if (halo and l > 0) else [], [kxg])
                  yield
                  if halo:
                      ts("dve", xgx[:, :, 0:n], xgx[:, :, 0:n], cflag[:, 0:1], None, ALU.mult, None, [kxg, "cflag"], [kxg])
                  act(sq8[:, :, 0:n], xgx[:, :, 0:n], AF.Square, [kxg], ["sq8"])
                  for kc in range(8):
                      mm(psx[:, 0:n], onesm[:], sq8[:, kc, 0:n], kc == 0, kc == 7, ["sq8", "onesm"], [kps])
                  yield
                  act(rs[:, 0:n], psx[:, 0:n], AF.Sqrt, [kps, "epsc"], [krs], bias=epsc[:, 0:1], scale=1.0)
                  yield
                  recip(rs[:, 0:n], rs[:, 0:n], [krs], [krs])
                  yield
                  for kc in range(8):
                      stt(hT[:, kc, hcols[0]:hcols[1]], xgx[:, kc, 0:n], colp[:, kc:kc + 1], rs[:, 0:n],
                          ALU.mult, ALU.mult, [kxg, "colp", krs], [key])
                      if kc % 2 == 1:
                          yield

              def p1_all():
                  for g in range(NG):
                      src = xs3[:, :, g * 512:(g + 1) * 512]
                      yield p1_chain(g, lambda q, o, src=src: q.dma_start(out=o, in_=src), 512,
                                     (g * 512, (g + 1) * 512), f"hT{g}")
                  if l == 0:
                      hsrc = xh_in.rearrange("(k p) t -> p k t", p=128)
                      yield p1_chain(NG, lambda q, o: q.dma_start(out=o, in_=hsrc), 32, (2048, 2080), "hTh", halo=True)
                  else:
                      yield p1_chain(NG, lambda q, o: q.dma_start(
                          out=o, in_=gTl.ap()[bass.ds(offs[("sp", "oT")], D), :].rearrange("(k p) t -> p k t", p=128)),
                          32, (2048, 2080), "hTh", halo=True)

              pipe(p1_all(), P1_DEPTH)
              if l == 0:
                  dump("hT", hT[:, :, 0:2048], [128, 8, 2048], BF16, [f"hT{g}" for g in range(NG)])

              chk(1, l)
              S.barrier()
              for vb in range(2):
                  S.op("dve", lambda q, vb=vb: q.memset(vstb[vb][:, :, :, 64:65], 1.0), [], [f"vst{vb}"])
              RG = [[0, 1, 2, 3], [4, 5, 6, 7]]
              GK = tuple(f"g{n}{c}" for n in "QKV" for c in range(4))

              def rr(*gens):
                  gens = list(gens)
                  while gens:
                      for gn in list(gens):
                          try:
                              next(gn)
                          except StopIteration:
                              gens.remove(gn)

              PQ = dict(n="q", tA=tA, tB=tB, tC=tC, tD=tD, tE=tE, xf=xfin, sm=sm, pT=pT, pTk="pT")
              PK = dict(n="k", tA=tA2, tB=tB2, tC=tC2, tD=tD2, tE=tE2, xf=xfin2, sm=sm2,
                        pT=pS[:, :].bitcast(BF16), pTk="pS")

              def norm_rope_gen(P, gain_lo, tile_i):
                  n = P["n"]
                  kA, kB, kC, kD, kE = (f"tA{n}", f"tB{n}", f"tC{n}", f"tD{n}", f"tE{n}")
                  smx = P["sm"]
                  X3 = P["tA"][:].rearrange("p (h d) -> p h d", h=8)
                  sq3 = P["tB"][:].rearrange("p (h d) -> p h d", h=8)
                  Xn = P["tC"][:].rearrange("p (h d) -> p h d", h=8)
                  xf = P["xf"]
                  tDn, tEn = P["tD"], P["tE"]
                  tt("pool" if n == "k" else "dve", sq3, X3, X3, ALU.mult, [kA], [kB])
                  yield
                  red(smx[:, 0:8], sq3, ALU.add, [kB], [f"sm0{n}"])
                  yield
                  act(smx[:, 16:24], smx[:, 0:8], AF.Sqrt, [f"sm0{n}", "epsc"], [f"sm16{n}"], bias=epsc[:, 0:1],
                      scale=1.0 / 96)
                  yield
                  recip(smx[:, 8:16], smx[:, 16:24], [f"sm16{n}"], [f"sm8{n}"])
                  yield
                  tt("dve", Xn, X3, smx[:, 8:16].unsqueeze(2).to_broadcast([128, 8, 96]), ALU.mult, [kA, f"sm8{n}"], [kC])
                  yield
                  tt("pool", Xn, Xn, rowb[:, gain_lo:gain_lo + 96].unsqueeze(1).to_broadcast([128, 8, 96]), ALU.mult,
                     [kC, "rowb"], [kC])
                  yield
                  cosb = cs_t[:, tile_i, 0:16].unsqueeze(1).to_broadcast([128, 8, 16])
                  sinb = cs_t[:, tile_i, 16:32].unsqueeze(1).to_broadcast([128, 8, 16])
                  x1 = Xn[:, :, 64:80]
                  x2 = Xn[:, :, 80:96]
                  cp("act", xf[:, :, 0:64], Xn[:, :, 0:64], [kC], [f"xf0{n}"])
                  tt("dve", tDn[:, 0], x1, cosb, ALU.mult, [kC, "cs_t"], [kD + "0"])
                  tt("pool", tEn[:, 0], x2, sinb, ALU.mult, [kC, "cs_t"], [kE + "0"])
                  yield
                  tt("dve", tDn[:, 1], x1, sinb, ALU.mult, [kC, "cs_t"], [kD + "1"])
                  tt("pool", tEn[:, 1], x2, cosb, ALU.mult, [kC, "cs_t"], [kE + "1"])
                  yield
                  tt("dve", xf[:, :, 64:80], tDn[:, 0], tEn[:, 0], ALU.subtract, [kD + "0", kE + "0"], [f"xf1{n}"])
                  yield
                  tt("dve", xf[:, :, 80:96], tDn[:, 1], tEn[:, 1], ALU.add, [kD + "1", kE + "1"], [f"xf2{n}"])
                  yield

              def transpose_gen(P, stg, i, stgkey):
                  n = P["n"]
                  xf = P["xf"]
                  pe_begin()
                  for h in range(8):
                      tr(P["pT"][0:96, h * 128:(h + 1) * 128], xf[:, h, :], ident_b[:],
                         [f"xf0{n}", f"xf1{n}", f"xf2{n}", "ident_b"], [P["pTk"]])
                  pe_end()
                  yield
                  cp("act", stg[0:96, :, i * 128:(i + 1) * 128],
                     P["pT"][0:96, :].rearrange("p (h t) -> p h t", h=8), [P["pTk"]], [stgkey])
                  yield

              def chain_q(g, i):
                  ti = g * 4 + i
                  tcs = slice(i * 128, (i + 1) * 128)
                  for ct in range(6):
                      mm(pA[:], cqTb[g % 2][:, ct, tcs], wuq[:, ct, 0:512], ct == 0, ct == 5, [f"cqT{g % 2}", "wuq"], ["pA"])
                  for ct in range(6):
                      mm(pB[:, 0:256], cqTb[g % 2][:, ct, tcs], wuq[:, ct, 512:768], ct == 0, ct == 5, [f"cqT{g % 2}", "wuq"], ["pB"])
                  yield
                  ts("dve", tA[:, 0:512], pA[:], rstdq[:, ti:ti + 1], None, ALU.mult, None, ["pA", f"rstdq{g}"], ["tAq"])
                  act(tA[:, 512:768], pB[:, 0:256], AF.Identity, ["pB", f"rstdq{g}"], ["tAq"], scale=rstdq[:, ti:ti + 1])
                  yield
                  yield from norm_rope_gen(PQ, 0, ti)
                  yield from transpose_gen(PQ, stgQ, i, "stgQ")

              def chain_k(g, i):
                  ti = g * 4 + i
                  tcs = slice(i * 128, (i + 1) * 128)
                  hk = f"hT{g}"
                  for ct in range(2):
                      mm(pC[:], ckvTb[g % 2][:, ct, tcs], wukv[:, ct, 0:512], ct == 0, ct == 1, [f"ckvT{g % 2}", "wukv"], ["pC"])
                  for ct in range(2):
                      mm(pD[:], ckvTb[g % 2][:, ct, tcs], wukv[:, ct, 512:1024], ct == 0, ct == 1, [f"ckvT{g % 2}", "wukv"], ["pD"])
                  for kc in range(8):
                      mm(pK[:, 0:32], hT[:, kc, g * 512 + i * 128:g * 512 + (i + 1) * 128], wbufB[:, kc, 256:288],
                         kc == 0, kc == 7, [hk, "wbufB"], ["pK"])
                  yield
                  kf = tA2[:].rearrange("p (h d) -> p h d", h=8)
                  pvC = pC[:].rearrange("p (h c) -> p h c", h=4)
                  pvD = pD[:].rearrange("p (h c) -> p h c", h=4)
                  rk = rstdkv[:, ti:ti + 1]
                  ts("dve", kf[:, 0:4, 0:64], pvC[:, :, 0:64], rk, None, ALU.mult, None, ["pC", f"rstdkv{g}"], ["tAk"])
                  act(kf[:, 4:8, 0:64], pvD[:, :, 0:64], AF.Identity, ["pD", f"rstdkv{g}"], ["tAk"], scale=rk)
                  yield
                  cp("dve", kf[:, :, 64:96], pK[:, 0:32].unsqueeze(1).to_broadcast([128, 8, 32]), ["pK"], ["tAk"])
                  vst = vstb[g % 2]
                  kvst = f"vst{g % 2}"
                  act(vst[:, 0:4, i, 0:64], pvC[:, :, 64:128], AF.Identity, ["pC", f"rstdkv{g}"], [kvst], scale=rk)
                  yield
                  ts("dve", vst[:, 4:8, i, 0:64], pvD[:, :, 64:128], rk, None, ALU.mult, None, ["pD", f"rstdkv{g}"], [kvst])
                  yield
                  yield from norm_rope_gen(PK, 96, ti)
                  yield from transpose_gen(PK, stgK, i, "stgK")

              def gl_chain(g):
                  gc = slice(g * 512, (g + 1) * 512)
                  hk = f"hT{g}"
                  cq, kcq, ckv, kckv = cqTb[g % 2], f"cqT{g % 2}", ckvTb[g % 2], f"ckvT{g % 2}"
                  for ct in range(6):
                      for kc in range(8):
                          mm(pX[:], wbufA[:, kc, ct * 128:(ct + 1) * 128], hT[:, kc, gc], kc == 0, kc == 7,
                             ["wbufA", hk], ["pX"])
                      act(cq[:, ct, :], pX[:], AF.Identity, ["pX", "colp"], [kcq], scale=colp[:, 16 + ct:17 + ct])
                      act(csq[:, ct, :], pX[:], AF.Square, ["pX"], ["csq"])
                      yield
                  for i in range(4):
                      for ct in range(6):
                          mm(pX[:, i:i + 1], csq[:, ct, i * 128:(i + 1) * 128], onescol[:, 1:2], ct == 0, ct == 5,
                             ["csq", "onescol"], ["pX"])
                  rsqrt(rstdq[:, g * 4:(g + 1) * 4], pX[:, 0:4], 1.0, ["pX", "epsc"], [f"rstdq{g}"],
                        tmp=sm[:, 24:28], tmpkey="sm24")
                  yield
                  for ct in range(2):
                      for kc in range(8):
                          mm(pX[:], wbufB[:, kc, ct * 128:(ct + 1) * 128], hT[:, kc, gc], kc == 0, kc == 7,
                             ["wbufB", hk], ["pX"])
                      act(ckv[:, ct, :], pX[:], AF.Identity, ["pX", "colp"], [kckv], scale=colp[:, 22 + ct:23 + ct])
                      act(csq[:, ct, :], pX[:], AF.Square, ["pX"], ["csq"])
                      yield
                  for i in range(4):
                      for ct in range(2):
                          mm(pX[:, i:i + 1], csq[:, ct, i * 128:(i + 1) * 128], onescol[:, 0:1], ct == 0, ct == 1,
                             ["csq", "onescol"], ["pX"])
                  rsqrt(rstdkv[:, g * 4:(g + 1) * 4], pX[:, 0:4], 1.0, ["pX", "epsc"], [f"rstdkv{g}"],
                        tmp=sm[:, 28:32], tmpkey="sm28")
                  yield

              def tiles(g):
                  for i in range(4):
                      act2 = [chain_q(g, i), chain_k(g, i)]
                      while act2:
                          for gn in list(act2):
                              try:
                                  next(gn)
                              except StopIteration:
                                  act2.remove(gn)
                          yield
                  dma("sp", sQ[g].ap().rearrange("(h d) t -> d h t", d=96), stgQ[0:96, :, :], "stgQ", ["stgQ"], [f"sQ{g}"])
                  dma("sp", sK[g].ap().rearrange("(h d) t -> d h t", d=96), stgK[0:96, :, :], "stgK", ["stgK"], [f"sK{g}"])
                  dma("sp", sV[g].ap().rearrange("(h p) (t c) -> p h t c", p=128, c=65), vstb[g % 2][:, :, :, :],
                      f"vst{g % 2}", [f"vst{g % 2}"], [f"sV{g}"])
                  if not (stop is not None and stop == (l, 2)):
                      for sname, s_t, g_t in (("Q", sQ[g], gQ[g]), ("K", sK[g], gK[g]), ("V", sV[g], gV[g])):
                          S.collective("pool", lambda q, s_t=s_t, g_t=g_t: q.collective_compute(
                              "AllGather", ALU.bypass, replica_groups=RG, ins=[s_t.ap().opt()], outs=[g_t.ap().opt()]),
                              [f"s{sname}{g}"], [f"g{sname}{g}"])
                  yield

              for _ in gl_chain(0):
                  pass
              for g in range(NG):
                  if g + 1 < NG:
                      rr(tiles(g), gl_chain(g + 1))
                  else:
                      rr(tiles(g))
                  if g == NG - 2:
                      dma("pool", wbufA[:, :, 0:768], win3[:, :, C_CONV[0]:C_CONV[1]], "wbufA", [], ["wbufA"])

              chk(2, l)
              chk(3, l)
              S.barrier(keep=GK)
              cv.off = shared0
              ybuf = cv.bf(2 * 2080).rearrange("p (c t) -> p c t", c=2)
              zc = cv.bf(2 * 2048).rearrange("p (c t) -> p c t", c=2)
              diag = cv.bf(62 * 128).rearrange("p (j m) -> p j m", j=62)
              c32 = cv.f32(2 * 512).rearrange("p (c t) -> p c t", c=2)
              cbf = cv.bf(2 * 512).rearrange("p (c t) -> p c t", c=2)
              csq2 = cv.bf(2 * 512).rearrange("p (c t) -> p c t", c=2)
              actv = cv.bf(2 * 512).rearrange("p (c t) -> p c t", c=2)
              yc32 = cv.f32(2 * 512).rearrange("p (c t) -> p c t", c=2)

              for ct in range(2):
                  for j in range(31):
                      col = 30 + ct * 31 + j
                      ts("dve", diag[:, ct * 31 + j, :], ident_b[:], colp[:, col:col + 1], None, ALU.mult, None,
                         ["ident_b", "colp"], ["diag"])
              spans = [("hTh", slice(2048, 2080), slice(0, 32), 32, None)]
              for g in range(NG):
                  spans.append((f"hT{g}", slice(g * 512, (g + 1) * 512), slice(32 + g * 512, 32 + (g + 1) * 512), 512, g))
              for hk, hcols, ycols, n, g in spans:
                  for ct in range(2):
                      for kc in range(8):
                          mm(pA[:, 0:n], wbufA[:, kc, ct * 128:(ct + 1) * 128], hT[:, kc, hcols], kc == 0, kc == 7,
                             ["wbufA", hk], ["pA"])
                      for kc in range(8):
                          mm(pB[:, 0:n], wbufA[:, kc, 256 + ct * 128:256 + (ct + 1) * 128], hT[:, kc, hcols], kc == 0,
                             kc == 7, ["wbufA", hk], ["pB"])
                      act(r512b[:, 0:n], pB[:, 0:n], AF.Sigmoid, ["pB"], ["r512b"])
                      tt("dve", ybuf[:, ct, ycols], pA[:, 0:n], r512b[:, 0:n], ALU.mult, ["pA", "r512b"], ["ybuf"])
                      if g is not None:
                          for kc in range(8):
                              mm(pC[:], wbufA[:, kc, 512 + ct * 128:512 + (ct + 1) * 128], hT[:, kc, hcols], kc == 0,
                                 kc == 7, ["wbufA", hk], ["pC"])
                          act(zc[:, ct, hcols], pC[:], AF.Silu, ["pC"], ["zc"])
              dma("pool", wbufB[:, :, 0:768], win3[:, :, C_SG[0]:C_SG[1]], "wbufB", [], ["wbufB"])
              dma("pool", wbufA[:, :, 0:512], win3[:, :, C_ZM[0]:C_ZM[1]], "wbufA", [], ["wbufA"])
              for g in range(NG):
                  gc = slice(g * 512, (g + 1) * 512)
                  for ct in range(2):
                      for j in range(31):
                          o = 32 + g * 512 - 30 + j
                          mm(pX[:], diag[:, ct * 31 + j, :], ybuf[:, ct, o:o + 512], j == 0, j == 30,
                             ["diag", "ybuf"], ["pX"])
                      act(c32[:, ct, :], pX[:], AF.Identity, ["pX", "colp"], ["c32"], bias=colp[:, 8 + ct:9 + ct])
                      act(csq2[:, ct, :], pX[:], AF.Square, ["pX", "colp"], ["csq2"], bias=colp[:, 8 + ct:9 + ct])
                      cp("dve", cbf[:, ct, :], c32[:, ct, :], ["c32"], ["cbf"])
                  for ct in range(2):
                      mm(pS[:], ones256[:], cbf[:, ct, :], ct == 0, ct == 1, ["ones256", "cbf"], ["pS"])
                  for ct in range(2):
                      mm(pK[:], ones256[:], csq2[:, ct, :], ct == 0, ct == 1, ["ones256", "csq2"], ["pK"])
                  cp("act", r512a[:], pS[:], ["pS"], ["r512a"])
                  tt("dve", r512b[:], r512a[:], r512a[:], ALU.mult, ["r512a"], ["r512b"])
                  tt("dve", r512b[:], pK[:], r512b[:], ALU.subtract, ["pK", "r512b"], ["r512b"])
                  ts("dve", r512b[:], r512b[:], 0.0, EPS, ALU.max, ALU.add, ["r512b"], ["r512b"])
                  act(r512c[:], r512b[:], AF.Sqrt, ["r512b"], ["r512c"])
                  recip(r512c[:], r512c[:], ["r512c"], ["r512c"])
                  for ct in range(2):
                      tt("dve", c32[:, ct, :], c32[:, ct, :], r512a[:], ALU.subtract, ["c32", "r512a"], ["c32"])
                      tt("dve", c32[:, ct, :], c32[:, ct, :], r512c[:], ALU.mult, ["c32", "r512c"], ["c32"])
                      act(actv[:, ct, :], c32[:, ct, :], AF.Silu, ["c32", "colp"], ["actv"],
                          bias=colp[:, 12 + ct:13 + ct], scale=colp[:, 10 + ct:11 + ct])
                  for co in range(2):
                      for ci in range(2):
                          mm(pA[:], pww[:, ci, co * 128:(co + 1) * 128], actv[:, ci, :], ci == 0, ci == 1,
                             ["pww", "actv"], ["pA"])
                      stt(yc32[:, co, :], pA[:], colp[:, 14 + co:15 + co], zc[:, co, gc], ALU.add, ALU.mult,
                          ["pA", "colp", "zc"], ["yc32"])
                      act(csq2[:, co, :], yc32[:, co, :], AF.Square, ["yc32"], ["csq2"])
                  for co in range(2):
                      mm(pS[:], ones256[:], csq2[:, co, :], co == 0, co == 1, ["ones256", "csq2"], ["pS"])
                  rsqrt(r512a[:], pS[:], 1.0, ["pS", "epsc"], ["r512a"])
                  for co in range(2):
                      stt(yT[:, co, gc], yc32[:, co, :], colp[:, 24 + co:25 + co], r512a[:], ALU.mult, ALU.mult,
                          ["yc32", "colp", "r512a"], [f"yT{g}"])

              chk(4, l)
              S.barrier(keep=GK)
              cvs = Carver()
              cvs.off = shared0
              QT = cvs.bf(SEQ)
              KT0 = cvs.bf(SEQ)
              Vstg = cvs.bf(64 * 65).rearrange("p (t c) -> p t c", c=65)
              ptb = [cvs.bf(1024) for _ in range(3)]
              KTb = [KT0, None]
              QTb = [QT, None]
              def load_q(hh):
                  for c in range(4):
                      S.dma("sp", lambda q, hh=hh, c=c: q.dma_start(
                          out=QTb[hh][0:96, :].rearrange("p (s c t) -> p s c t", s=4, c=4)[:, :, c, :],
                          in_=gQ[c].ap().rearrange("(s r) t -> r s t", s=4)[hh * 96:768, :, :][
                              bass.ds(offs[("sp", "o192")], 96), :, :]),
                          "QT" if hh == 0 else "QT1", [f"gQ{c}"], ["QT" if hh == 0 else "QT1"])

              def load_k(hh, eng):
                  for c in range(4):
                      S.dma(eng, lambda q, hh=hh, c=c: q.dma_start(
                          out=KTb[hh][0:96, :].rearrange("p (s c t) -> p s c t", s=4, c=4)[:, :, c, :],
                          in_=gK[c].ap().rearrange("(s r) t -> r s t", s=4)[hh * 96:768, :, :][
                              bass.ds(offs[(eng, "o192")], 96), :, :]),
                          f"KT{hh}", [f"gK{c}"], [f"KT{hh}"])

              def load_v(hh, eng):
                  for c in range(4):
                      S.dma(eng, lambda q, hh=hh, c=c: q.dma_start(
                          out=Vstg[:, :, :].rearrange("p (s c t) x -> p s c (t x)", s=4, c=4)[:, :, c, :],
                          in_=gV[c].ap().rearrange("(s r) x -> r s x", s=4)[hh * 128:1024, :, :][
                              bass.ds(offs[(eng, "o256")], 128), :, :]),
                          "Vstg", [f"gV{c}"], ["Vstg"])

              load_k(0, "sp")
              load_v(0, "sp")
              load_q(0)

              xk = ["xfin0", "xfin1", "xfin2"]

              SG = [
                  dict(n="0", pUV=pA, kUV="pA", pZM=pB, kZM="pB", pTT=pT, kTT="pT", pG=pX, kG="pX",
                       ra=r512a, rb=r512b, rc=r512c, tB=tB, tC=tC, xf=xfin, sm=sm[:, 32:48], zst=tA[:, 0:256].bitcast(BF16)),
                  dict(n="1", pUV=pC, kUV="pC", pZM=pD, kZM="pD", pTT=pK[:, :].bitcast(BF16), kTT="pK", pG=pS, kG="pS",
                       ra=r512d, rb=r512e, rc=r512f, tB=tB2, tC=tC2, xf=xfin2, sm=sm2[:, 0:16],
                       zst=tA2[:, 0:256].bitcast(BF16)),
              ]

              def sgu_chain(ti):
                  P = SG[ti % 2]
                  n = P["n"]
                  g = ti // 4
                  tc = slice(ti * 128, (ti + 1) * 128)
                  hk = f"hT{g}"
                  pUV, kUV, pZM, kZM, pTT, kTT, pG, kG = (P[k] for k in ("pUV", "kUV", "pZM", "kZM", "pTT", "kTT", "pG", "kG"))
                  ra, rb, rc, tBn, tCn, smx = P["ra"], P["rb"], P["rc"], P["tB"], P["tC"], P["sm"]
                  kra, krb, krc = ("r512a", "r512b", "r512c") if n == "0" else ("r512d", "r512e", "r512f")
                  ktB, ktC = f"tB{n}", f"tC{n}"
                  xf2 = P["xf"][:].rearrange("p h d -> p (h d)")
                  vln = xf2[:, 0:256]
                  ysg = xf2[:, 256:512]
                  kx = f"xf{n}"
                  for kc in range(8):
                      mm(pUV[:, :], hT[:, kc, tc], wbufB[:, kc, 0:512], kc == 0, kc == 7, [hk, "wbufB"], [kUV])
                  for kc in range(8):
                      mm(pZM[:, 0:256], hT[:, kc, tc], wbufB[:, kc, 512:768], kc == 0, kc == 7, [hk, "wbufB"], [kZM])
                  for kc in range(8):
                      mm(pG[:, :], hT[:, kc, tc], wbufA[:, kc, 0:512], kc == 0, kc == 7, [hk, "wbufA"], [kG])
                  yield
                  act(ra[:], pUV[:, :], AF.Square, [kUV], [kra])
                  yield
                  ts("dve", ra[:], ra[:], 0.044715, 1.0, ALU.mult, ALU.add, [kra], [kra])
                  yield
                  tt("dve", ra[:], ra[:], pUV[:, :], ALU.mult, [kra, kUV], [kra])
                  yield
                  act(ra[:], ra[:], AF.Sigmoid, [kra], [kra], scale=GELU_C)
                  yield
                  tt("dve", rb[:], ra[:], pUV[:, :], ALU.mult, [kra, kUV], [krb])
                  act(rc[:, 0:256], pZM[:, 0:256], AF.Silu, [kZM], [krc])
                  yield
                  act(P["zst"], pG[:, :], AF.Silu, [kG], [f"zst{n}"])
                  dma("sp", zgd.ap()[ti * 128:(ti + 1) * 128, :], P["zst"], f"zst{n}", [f"zst{n}"], ["zgd"])
                  S.op("dve", lambda q: q.bn_stats(smx[:, 0:6], rb[:, 256:512]), [krb], [f"sgs0{n}"])
                  yield
                  S.op("dve", lambda q: q.bn_aggr(smx[:, 8:10], smx[:, 0:6]), [f"sgs0{n}"], [f"sgs8{n}"])
                  yield
                  act(smx[:, 11:12], smx[:, 9:10], AF.Sqrt, [f"sgs8{n}", "epsc"], [f"sgs11{n}"], bias=epsc[:, 0:1], scale=1.0)
                  yield
                  recip(smx[:, 10:11], smx[:, 11:12], [f"sgs11{n}"], [f"sgs10{n}"])
                  yield
                  ts("dve", tBn[:, 0:256], rb[:, 256:512], smx[:, 8:9], smx[:, 10:11], ALU.subtract, ALU.mult,
                     [krb, f"sgs8{n}", f"sgs10{n}"], [ktB])
                  yield
                  tt("pool", tBn[:, 0:256], tBn[:, 0:256], rowb[:, 192:448], ALU.mult, [ktB, "rowb"], [ktB])
                  yield
                  tt("dve", vln, tBn[:, 0:256], rowb[:, 448:704], ALU.add, [ktB, "rowb", kx], [kx])
                  yield
                  pe_begin()
                  for gg in range(4):
                      mm(pZM[:, 256 + gg * 64:256 + (gg + 1) * 64], sgw[:, gg, :], vln[:, gg * 64:(gg + 1) * 64], True, True,
                         ["sgw", kx], [kZM])
                  pe_end()
                  yield
                  tt("dve", tBn[:, 256:512].rearrange("p (g d) -> p g d", g=4),
                     pZM[:, 256:512].rearrange("p (g d) -> p g d", g=4),
                     colp[:, 26:30].unsqueeze(2).to_broadcast([128, 4, 64]), ALU.add, [kZM, "colp"], [ktB])
                  yield
                  tt("pool", tBn[:, 256:512], tBn[:, 256:512], rb[:, 0:256], ALU.mult, [ktB, krb], [ktB])
                  yield
                  tt("dve", tBn[:, 256:512], tBn[:, 256:512], rc[:, 0:256], ALU.mult, [ktB, krc], [ktB])
                  yield
                  tt("pool", tCn[:, 0:256], tBn[:, 256:512], tBn[:, 256:512], ALU.mult, [ktB], [ktC])
                  yield
                  red(smx[:, 12:13], tCn[:, 0:256], ALU.add, [ktC], [f"sgs12{n}"])
                  yield
                  act(smx[:, 14:15], smx[:, 12:13], AF.Sqrt, [f"sgs12{n}", "epsc"], [f"sgs14{n}"], bias=epsc[:, 0:1],
                      scale=1.0 / 256)
                  yield
                  recip(smx[:, 13:14], smx[:, 14:15], [f"sgs14{n}"], [f"sgs13{n}"])
                  yield
                  stt(ysg, tBn[:, 256:512], smx[:, 13:14], rowb[:, 1216:1472], ALU.mult, ALU.mult,
                      [ktB, f"sgs13{n}", "rowb", kx], [kx])
                  yield
                  pe_begin()
                  for c in range(2):
                      tr(pTT[:, c * 128:(c + 1) * 128], ysg[:, c * 128:(c + 1) * 128], ident_b[:], [kx, "ident_b"], [kTT])
                  pe_end()
                  yield
                  cp("act", yT[:, 6:8, tc], pTT[:, 0:256].rearrange("p (c t) -> p c t", c=2), [kTT], [f"yT{g}"])
                  yield

              pipe((sgu_chain(ti) for ti in range(NT)), SGU_DEPTH)
              chk(5, l)

              if l == 0:
                  dump("yT_a", yT[:, :, :], [128, 8, T], BF16, [f"yT{g}" for g in range(NG)])

              chk(6, l)
              S.barrier(keep=GK + ("QT", "KT0", "Vstg"))
              cv = Carver()
              KTb = [KT0, cv.bf(SEQ)]
              Vpad = cv.bf(64 * 128).rearrange("p (t c) -> p t c", c=128)
              otb = [cv.f32(512) for _ in range(2)]
              o_all = cv.bf(64 * 128).rearrange("p (h t c) -> p h t c", h=2, c=64)
              endB = cv.off
              QT1 = cv.bf(SEQ)
              cv.off -= SEQ
              wout = cv.bf(8 * 1024).rearrange("p (k c) -> p k c", k=8)
              assert cv.off <= shared0
              QTb = [QT, QT1]
              load_q(1)
              zbufs = [tA, tB, tC, tA2, tB2, tC2]
              zgt = [zbufs[ti // 3][:, (ti % 3) * 256:(ti % 3 + 1) * 256].bitcast(BF16) for ti in range(NT)]
              spair = [(pAB, "pAB"), (pCD, "pCD")]
              obanks = [(pX, "pX"), (pK, "pK")]
              gpair = [0]

              S.op("pool", lambda q: q.memset(Vpad[:, :, 64:128], 0.0), [], ["Vpad"])
              for ti in range(NT):
                  dma("sp", zgt[ti], zgd.ap()[ti * 128:(ti + 1) * 128, :], "zgt", ["zgd"], [f"zgt{ti}"])
              for hh in range(2):
                  KT = KTb[hh]
                  kKT = f"KT{hh}"
                  QTx = QTb[hh]
                  kQT = "QT" if hh == 0 else "QT1"
                  Vh = Vpad
                  kVh = "Vpad"
                  cp("dve", Vpad[:, :, 0:65], Vstg[:, :, :], ["Vstg"], ["Vpad"])
                  if hh == 0:
                      load_k(1, "act")
                      load_v(1, "pool")
                  pairs = []
                  for qg in range(16):
                      nk = 4 * qg + 4
                      for kp in range(nk // 2):
                          pairs.append((qg, kp, nk))
                  pend = []

                  def emit_qk(pi, hh=hh, KT=KT, kKT=kKT, QTx=QTx, kQT=kQT):
                      qg, kp, nk = pairs[pi]
                      sb2, sbk = spair[gpair[0] % 2]
                      pt = ptb[gpair[0] % 3]
                      ptk = f"pt{gpair[0] % 3}"
                      gpair[0] += 1
                      pe_begin()
                      for h2 in range(2):
                          kt = 2 * kp + h2
                          mm(sb2[:, h2 * 512:(h2 + 1) * 512], KT[0:96, kt * 128:(kt + 1) * 128],
                             QTx[0:96, qg * 512:(qg + 1) * 512], True, True, [kKT, kQT], [sbk])
                      pe_end()
                      act(pt, sb2[:, :], AF.Exp, [sbk, "negM"], [ptk], bias=negM[:, 0:1], scale=SCALE)
                      if 2 * kp >= 4 * qg:
                          i0 = 2 * kp - 4 * qg
                          tt("dve", pt, pt, masks_b[:, i0:i0 + 2, :].rearrange("p a b -> p (a b)"), ALU.mult,
                             [ptk, "masks_b"], [ptk])
                      return pt, ptk

                  def emit_pv(pi, pt, ptk, hh=hh, Vh=Vh, kVh=kVh):
                      qg, kp, nk = pairs[pi]
                      ob, obk = obanks[qg % 2]
                      pe_begin()
                      for h2 in range(2):
                          kt = 2 * kp + h2
                          mm(ob[:, :], Vh[:, kt, :], pt[:, h2 * 512:(h2 + 1) * 512], kt == 0, kt == nk - 1,
                             [kVh, ptk], [obk])
                      pe_end()
                      if kp == nk // 2 - 1:
                          ot = otb[qg % 2]
                          otk = f"ot{qg % 2}"
                          cp("dve", ot[0:65, :], ob[0:65, :], [obk], [otk])

                          def fin(qg=qg, ot=ot, otk=otk, hh=hh):
                              pe_begin()
                              for i in range(4):
                                  tr(pS[:, i * 65:(i + 1) * 65], ot[0:65, i * 128:(i + 1) * 128], ident_f[0:65, 0:65],
                                     [otk, "ident_f"], ["pS"])
                              pe_end()
                              pv = pS[:, 0:260].rearrange("p (i c) -> p i c", c=65)
                              recip(sm[:, 48:52].unsqueeze(2), pv[:, :, 64:65], ["pS"], ["sm48"])
                              okey = "o_all0" if hh == 0 else ("o_all1a" if qg < 12 else "o_all1b")
                              tt("dve", o_all[:, hh, qg * 4:(qg + 1) * 4, :], pv[:, :, 0:64],
                                 sm[:, 48:52].unsqueeze(2).to_broadcast([128, 4, 64]), ALU.mult, ["pS", "sm48"],
                                 [okey])
                              if hh == 1 and qg == 11 and not (stop is not None and stop == (l, 7)):
                                  for t3 in range(3):
                                      dma("sp", sO1a.ap()[:, t3 * 1024:(t3 + 1) * 1024],
                                          o_all[:, 1, t3 * 16:(t3 + 1) * 16, :].rearrange("p t c -> p (t c)"),
                                          "o_all1a", ["o_all1a"], ["sO1a"])
                                  S.collective("pool", lambda q: q.collective_compute(
                                      "AllGather", ALU.bypass, replica_groups=RG, ins=[sO1a.ap().opt()],
                                      outs=[gO1a.ap().opt()]), ["sO1a"], ["gO1a"])
                          pend.append([2, fin])

                  prev = None
                  for pi in range(len(pairs)):
                      cur = emit_qk(pi)
                      if prev is not None:
                          emit_pv(pi - 1, *prev)
                      prev = cur
                      for it in list(pend):
                          it[0] -= 1
                          if it[0] <= 0:
                              it[1]()
                              pend.remove(it)
                  emit_pv(len(pairs) - 1, *prev)
                  for it in pend:
                      it[1]()
                  if stop is not None and stop == (l, 7):
                      continue
                  if hh == 0:
                      for t4 in range(4):
                          dma("sp", sO[0].ap()[:, t4 * 1024:(t4 + 1) * 1024],
                              o_all[:, 0, t4 * 16:(t4 + 1) * 16, :].rearrange("p t c -> p (t c)"), "o_all0",
                              ["o_all0"], ["sO0"])
                      S.collective("pool", lambda q: q.collective_compute(
                          "AllGather", ALU.bypass, replica_groups=RG, ins=[sO[0].ap().opt()], outs=[gO[0].ap().opt()]),
                          ["sO0"], ["gO0"])
                  else:
                      dma("sp", sO1b.ap()[:, :], o_all[:, 1, 48:64, :].rearrange("p t c -> p (t c)"), "o_all1b",
                          ["o_all1b"], ["sO1b"])
                      S.collective("pool", lambda q: q.collective_compute(
                          "AllGather", ALU.bypass, replica_groups=RG, ins=[sO1b.ap().opt()], outs=[gO1b.ap().opt()]),
                          ["sO1b"], ["gO1b"])
              dma("pool", wout[:], w_out[l].rearrange("(k p) c -> p k c", p=128), "wout", [], ["wout", "QT1"])
              chk(7, l)

              chk(8, l)
              S.barrier(keep=("wout",) + tuple(f"zgt{ti}" for ti in range(NT)))
              cv = Carver()
              xg2 = [cv.f32(8 * 512).rearrange("p (k t) -> p k t", k=8) for _ in range(2)]
              att = cv.bf(16 * 512).rearrange("p (a b t c) -> p a b t c", a=4, b=2, c=64)
              assert cv.off <= endB
              cvb = Carver()
              cvb.off = shared0
              tmpB = cvb.bf(4 * 1024).rearrange("p (a x) -> p a x", a=4)
              for hp in range(4):
                  S.dma("act", lambda q, hp=hp: q.dma_start(
                      out=att[:, hp, 0, :, :].rearrange("p t c -> p (t c)"),
                      in_=gO[0].ap()[hp * 128:(hp + 1) * 128, :][:, bass.ds(offs[("act", "o1024")], 1024)]),
                      "att", ["gO0"], ["att"])
              for hp in range(4):
                  S.dma("act", lambda q, hp=hp: q.dma_start(
                      out=att[:, hp, 1, :, :].rearrange("p t c -> p (t c)"),
                      in_=gO1a.ap()[hp * 128:(hp + 1) * 128, :][:, bass.ds(offs[("act", "oA")], 1024)]),
                      "att", ["gO1a"], ["att"])
              for hp in range(4):
                  dma("sp", tmpB[:, hp, :], gO1b.ap()[hp * 128:(hp + 1) * 128, :], "tmpB", ["gO1b"], ["tmpB"])
              att1v = att[:, :, 1, :, :].rearrange("p a t c -> p a (t c)")
              ts("dve", att1v, att1v, ofl[:, 0:1], None, ALU.mult, None, ["att", "ofl"], ["att"])
              stt(att1v, tmpB[:, :, :], ofl[:, 1:2], att1v, ALU.mult, ALU.add, ["tmpB", "ofl", "att"], ["att"])
              EP = [
                  dict(n="0", ra=r512a, rb=r512b, kra="r512a", krb="r512b", xf=xfin, kx="xfin", sm=sm[:, 52:56],
                       pTT=pT, kTT="pT"),
                  dict(n="1", ra=r512d, rb=r512e, kra="r512d", krb="r512e", xf=xfin2, kx="xfin2", sm=sm2[:, 16:20],
                       pTT=pK[:, :].bitcast(BF16), kTT="pK"),
              ]

              def epi_chain(ti):
                  P = EP[(ti % 2) * EPI_SETS]
                  n = P["n"]
                  g = ti // 4
                  tc = slice(ti * 128, (ti + 1) * 128)
                  ra, rb, kra, krb, kx, smx, pTT, kTT = (P[k] for k in ("ra", "rb", "kra", "krb", "kx", "sm", "pTT", "kTT"))
                  yo = P["xf"][:].rearrange("p h d -> p (h d)")[:, 0:512]
                  tt("dve", ra[:].rearrange("p (a b c) -> p a b c", a=4, b=2), att[:, :, :, ti, :],
                     zgt[ti].rearrange("p (a b c) -> p a b c", a=4, b=2), ALU.mult, ["att", f"zgt{ti}"], [kra])
                  yield
                  tt("pool", rb[:], ra[:], ra[:], ALU.mult, [kra], [krb])
                  yield
                  red(smx[:, 0:1], rb[:], ALU.add, [krb], [f"sme0{n}"])
                  yield
                  act(smx[:, 2:3], smx[:, 0:1], AF.Sqrt, [f"sme0{n}", "epsc"], [f"sme2{n}"], bias=epsc[:, 0:1],
                      scale=1.0 / 512)
                  yield
                  recip(smx[:, 1:2], smx[:, 2:3], [f"sme2{n}"], [f"sme1{n}"])
                  yield
                  stt(yo, ra[:], smx[:, 1:2], rowb[:, 704:1216], ALU.mult, ALU.mult, [kra, f"sme1{n}", "rowb", kx], [kx])
                  yield
                  pe_begin()
                  for c in range(4):
                      tr(pTT[:, c * 128:(c + 1) * 128], yo[:, c * 128:(c + 1) * 128], ident_b[:], [kx, "ident_b"], [kTT])
                  pe_end()
                  yield
                  cp("act", yT[:, 2:6, tc], pTT[:, 0:512].rearrange("p (c t) -> p c t", c=4), [kTT], [f"yT{g}"])
                  yield

              obl = [(pA, "pA"), (pB, "pB"), (pC, "pC"), (pD, "pD")]

              def outproj_chain(g):
                  gc = slice(g * 512, (g + 1) * 512)
                  xb = xg2[g % 2]
                  kxb = f"xg2_{g % 2}"
                  dma("sp", xb[:], xs3[:, :, gc], kxb, ["xres"] if l > 0 else [], [kxb])
                  yield
                  for co in range(8):
                      pb, pbk = obl[co % 4]
                      for kc in range(8):
                          mm(pb[:, :], wout[:, kc, co * 128:(co + 1) * 128], yT[:, kc, gc], kc == 0, kc == 7,
                             ["wout", f"yT{g}"], [pbk])
                      tt("dve", xb[:, co, :], pb[:, :], xb[:, co, :], ALU.add, [pbk, kxb], [kxb])
                      yield
                  dma("sp", xd3[:, :, gc], xb[:], kxb, [kxb], ["xres" if l < depth - 1 else "outT"])
                  if l < depth - 1 and g == NG - 1:
                      dma("sp", sTl.ap().rearrange("(k p) t -> p k t", p=128), xb[:, :, 480:512], kxb, [kxb], ["sTl"])
                  yield

              def epi_all():
                  for g in range(NG):
                      for i in range(4):
                          yield epi_chain(g * 4 + i)
                          if i == 0 and g > 0:
                              yield outproj_chain(g - 1)
                  yield None
                  yield outproj_chain(NG - 1)

              pipe(epi_all(), EPI_DEPTH)
              if l == 0:
                  dump("yT_b", yT[:, :, :], [128, 8, T], BF16, [f"yT{g}" for g in range(NG)])
                  dump("xres", xres.ap(), [D, T], F32, ["xres"])
              if l < depth - 1:
                  S.collective("pool", lambda q: q.collective_compute(
                      "AllGather", ALU.bypass, replica_groups=RG, ins=[sTl.ap().opt()], outs=[gTl.ap().opt()]),
                      ["sTl"], ["gTl"])

        except _Stop:
            pass
        keys = ["outT"] + ["dbg_" + k for k in dbg]
        assert not pe_pend and pe_depth[0] == 0
        S.wait_all("sp", keys)
        S.barrier()
        S.emit()
    return nc, list(dbg.keys())


def _host_consts():
    ident = np.eye(128, dtype=np.float32)
    k = np.arange(128)[:, None]
    q = np.arange(512)[None, :]
    masks = np.concatenate([(q >= 128 * i + k).astype(np.float32) for i in range(4)], axis=1)
    return np.ascontiguousarray(np.concatenate([ident, masks], axis=1))


def _rope_table(pos0):
    half = 16
    inv_freq = (np.float32(10000.0) ** (-np.arange(half, dtype=np.float32) / np.float32(half))).astype(np.float32)
    pos = (pos0 + np.arange(T, dtype=np.float32)).astype(np.float32)
    ang = (pos[:, None] * inv_freq[None, :]).astype(np.float32)
    cs = np.concatenate([np.cos(ang), np.sin(ang)], axis=1).astype(np.float32)
    return np.ascontiguousarray(cs.reshape(NT, 128, 32).transpose(1, 0, 2).reshape(128, NT * 32))


def _pack_params(p):
    L = 2
    colp = np.zeros((L, 128, NCOL), np.float32)
    rowp = np.zeros((L, NROW), np.float32)

    def cols(v, n):
        return np.asarray(v, np.float32).reshape(n, 128).T

    for l in range(L):
        colp[l, :, 0:8] = cols(p["norm_g"][l], 8)
        colp[l, :, 8:10] = cols(p["conv_b"][l], 2)
        colp[l, :, 10:12] = cols(p["conv_ln_g"][l], 2)
        colp[l, :, 12:14] = cols(p["conv_ln_b"][l], 2)
        colp[l, :, 14:16] = cols(p["conv_pw_b"][l], 2)
        colp[l, :, 16:22] = cols(p["q_norm_g"][l], 6)
        colp[l, :, 22:24] = cols(p["kv_norm_g"][l], 2)
        colp[l, :, 24:26] = cols(p["branch_norm_g"][l][0:256], 2)
        colp[l, :, 26:30] = np.asarray(p["sg_b"][l], np.float32).T
        cw = np.asarray(p["conv_w"][l], np.float32)
        for ct in range(2):
            colp[l, :, 30 + ct * 31:30 + (ct + 1) * 31] = cw[:, ct * 128:(ct + 1) * 128].T
        rowp[l, 0:96] = p["qk_q_g"][l]
        rowp[l, 96:192] = p["qk_k_g"][l]
        rowp[l, 192:448] = p["sg_ln_g"][l]
        rowp[l, 448:704] = p["sg_ln_b"][l]
        rowp[l, 704:1216] = p["branch_norm_g"][l][256:768]
        rowp[l, 1216:1472] = p["branch_norm_g"][l][768:1024]
    return colp, rowp


_CACHE = {}


def make_in_maps(inputs):
    p = {k: np.asarray(v) for k, v in inputs.items()}
    x = p["x"].astype(np.float32)
    colp, rowp = _pack_params(p)
    cst = _host_consts()
    sgwT = np.ascontiguousarray(np.transpose(p["sg_w"].astype(np.float32), (0, 3, 1, 2)).reshape(2, 128, 512))
    shared = {
        "w_in": np.ascontiguousarray(p["w_in"], dtype=np.float32),
        "w_uq": np.ascontiguousarray(p["w_uq"], dtype=np.float32),
        "w_ukv": np.ascontiguousarray(p["w_ukv"], dtype=np.float32),
        "w_out": np.ascontiguousarray(p["w_out"], dtype=np.float32),
        "pw": np.ascontiguousarray(p["conv_pw_w"], dtype=np.float32),
        "sgwT": sgwT, "colp": colp, "rowp": rowp, "cst": cst,
    }
    maps = []
    for c in range(NCORES):
        b, j = c // 4, c % 4
        xs = x[b, j * T:(j + 1) * T, :]
        m = dict(shared)
        m["xT"] = np.ascontiguousarray(xs.T)
        if j == 0:
            m["xh"] = np.zeros((D, 32), np.float32)
        else:
            m["xh"] = np.ascontiguousarray(x[b, j * T - 32:j * T, :].T)
        m["posoff"] = np.full((128, 1), float(j * T), np.float32)
        m["idx"] = np.array([[j * 192, j * 256, j * T, max(j - 1, 0) * D, j * 1024, min(j, 2) * 1024, 0, 0]], dtype=np.int32)
        m["ofl"] = np.tile(np.array([[1.0, 0.0]] if j < 3 else [[0.0, 1.0]], np.float32), (128, 1))
        m["cflag"] = np.full((128, 1), 1.0 if j > 0 else 0.0, np.float32)
        maps.append(m)
    return maps


def kernel(**inputs):
    if "nc" not in _CACHE:
        _CACHE["nc"] = build_program(depth=2, debug=False)[0]
    nc = _CACHE["nc"]
    maps = make_in_maps(inputs)
    res = run_bass_kernel_spmd(nc, maps, core_ids=list(range(NCORES)))
    out = np.empty((2, SEQ, D), np.float32)
    for c in range(NCORES):
        b, j = c // 4, c % 4
        out[b, j * T:(j + 1) * T, :] = np.asarray(res.results[c]["outT"]).T
    return out
```
